# Optimizing a Trainium2 kernel written in Bass

```python
import jax, jax.numpy as jnp
from jax import lax
import numpy as np

D_MODEL = 1024
BATCH = 2
SEQ = 16384
DEPTH = 4

N_MIXERS = 2
N_A_LAYERS = (DEPTH + 1) // 2
N_B_LAYERS = DEPTH // 2
RMS_EPS = 1e-6
NEG_INF = -1e30

A_GROUPS = ((128, 1), (512, 4), (2048, 16))
A_N_GROUPS = len(A_GROUPS)
A_HEADS = 16
A_HEAD_DIM = D_MODEL // A_HEADS
A_WIDTH = A_HEADS * A_HEAD_DIM
A_IN_WIDTH = A_N_GROUPS * 3 * A_WIDTH
ROPE_THETA = 10000.0

B_HEADS = 4
B_KEY_DIM = D_MODEL // 2 // B_HEADS
B_VAL_DIM = D_MODEL // B_HEADS
B_QK_WIDTH = B_HEADS * B_KEY_DIM
B_V_WIDTH = B_HEADS * B_VAL_DIM
B_GATE_RANK = 16
B_GATE_TAU = 16.0
B_CHUNK = 64
B_IN_WIDTH = 2 * B_QK_WIDTH + 2 * B_V_WIDTH + 2 * B_GATE_RANK

FFN_HIDDEN = -(-8 * D_MODEL // (3 * 256)) * 256

kernel_name = "hybrid_dilated_attn_gla_encoder"


def rms_norm(x, gain):
    xf = x.astype(jnp.float32)
    y = xf * lax.rsqrt(jnp.mean(xf * xf, axis=-1, keepdims=True) + RMS_EPS)
    return (y * gain.astype(jnp.float32)).astype(x.dtype)


def rope(x, positions):
    half = x.shape[-1] // 2
    inv_freq = ROPE_THETA ** (-jnp.arange(half, dtype=jnp.float32) / half)
    ang = positions.astype(jnp.float32)[:, None] * inv_freq[None, :]
    cos = jnp.cos(ang)[:, None, :]
    sin = jnp.sin(ang)[:, None, :]
    xf = x.astype(jnp.float32)
    x1, x2 = xf[..., :half], xf[..., half:]
    return jnp.concatenate([x1 * cos - x2 * sin, x2 * cos + x1 * sin], axis=-1).astype(x.dtype)


def dilated_window_attention(q, k, v, window, dilation):
    bsz, seq, nh, dh = q.shape
    half = window // (2 * dilation)
    L = seq // dilation
    nb = -(-L // half)
    Lp = nb * half

    def to_phase(t):
        t = t.reshape(bsz, L, dilation, nh, dh)
        return jnp.moveaxis(t, 2, 1).reshape(bsz * dilation, L, nh, dh)

    n = bsz * dilation
    qp = jnp.pad(to_phase(q), ((0, 0), (0, Lp - L), (0, 0), (0, 0))).reshape(n, nb, half, nh, dh)

    def key_blocks(t):
        t = jnp.pad(to_phase(t), ((0, 0), (half, Lp - L + half), (0, 0), (0, 0)))
        t = t.reshape(n, nb + 2, half, nh, dh)
        return jnp.concatenate([t[:, :-2], t[:, 1:-1], t[:, 2:]], axis=2)

    kb = key_blocks(k)
    vb = key_blocks(v)
    tq = jnp.arange(nb)[:, None] * half + jnp.arange(half)[None, :]
    tk = jnp.arange(nb)[:, None] * half + jnp.arange(3 * half)[None, :] - half
    dist = tk[:, None, :] - tq[:, :, None]
    valid = (tk[:, None, :] >= 0) & (tk[:, None, :] < L) & (jnp.abs(dist) <= half)

    scores = jnp.einsum("nbqhd,nbkhd->nbhqk", qp.astype(jnp.float32), kb.astype(jnp.float32)) * (dh ** -0.5)
    scores = jnp.where(valid[None, :, None], scores, NEG_INF)
    m = jnp.max(scores, axis=-1, keepdims=True)
    p = jnp.exp(scores - m)
    l = jnp.sum(p, axis=-1)
    o = jnp.einsum("nbhqk,nbkhd->nbqhd", p, vb.astype(jnp.float32))
    o = o / jnp.moveaxis(l, 2, 3)[..., None]
    lse = jnp.moveaxis(m[..., 0] + jnp.log(l), 2, 3)

    def from_phase(t):
        rest = t.shape[4:]
        t = t.reshape(bsz, dilation, Lp, nh, *rest)[:, :, :L]
        return jnp.moveaxis(t, 1, 2).reshape(bsz, seq, nh, *rest)

    return from_phase(o), from_phase(lse)


def dilated_attention_mixer(h, w_in, q_gain, k_gain, w_out, positions):
    bsz, seq, _ = h.shape
    qkv = (h @ w_in).reshape(bsz, seq, A_N_GROUPS, 3, A_HEADS, A_HEAD_DIM)
    outs, lses = [], []
    for g, (window, dilation) in enumerate(A_GROUPS):
        q = rope(rms_norm(qkv[:, :, g, 0], q_gain[g]), positions)
        k = rope(rms_norm(qkv[:, :, g, 1], k_gain[g]), positions)
        v = qkv[:, :, g, 2]
        o, lse = dilated_window_attention(q, k, v, window, dilation)
        outs.append(o)
        lses.append(lse)
    alpha = jax.nn.softmax(jnp.stack(lses, axis=0), axis=0)
    out = jnp.sum(alpha[..., None] * jnp.stack(outs, axis=0), axis=0)
    return out.reshape(bsz, seq, A_WIDTH).astype(h.dtype) @ w_out


def gla_chunk(q, k, v, log_a, strict):
    bsz, nh, seq, dk = q.shape
    dv = v.shape[-1]
    nc = seq // B_CHUNK
    q = q.astype(jnp.float32).reshape(bsz, nh, nc, B_CHUNK, dk)
    k = k.astype(jnp.float32).reshape(bsz, nh, nc, B_CHUNK, dk)
    v = v.astype(jnp.float32).reshape(bsz, nh, nc, B_CHUNK, dv)
    b = jnp.cumsum(log_a.astype(jnp.float32).reshape(bsz, nh, nc, B_CHUNK, dk), axis=3)
    b_last = b[..., -1:, :]
    q_t = q * jnp.exp(b)
    k_t = k * jnp.exp(-b)
    k_end = k * jnp.exp(b_last - b)
    mask = jnp.tril(jnp.ones((B_CHUNK, B_CHUNK), dtype=bool), k=-1 if strict else 0)
    attn = jnp.where(mask, jnp.einsum("bhncd,bhnsd->bhncs", q_t, k_t), 0.0)
    o_intra = jnp.einsum("bhncs,bhnse->bhnce", attn, v)
    chunk_kv = jnp.einsum("bhncd,bhnce->bhnde", k_end, v)
    decay = jnp.exp(b_last[..., 0, :])

    def step(state, inp):
        kv_n, dec_n = inp
        return dec_n[..., None] * state + kv_n, state

    init = jnp.zeros((bsz, nh, dk, dv), jnp.float32)
    _, s_in = lax.scan(step, init, (jnp.moveaxis(chunk_kv, 2, 0), jnp.moveaxis(decay, 2, 0)))
    s_in = jnp.moveaxis(s_in, 0, 2)
    o_inter = jnp.einsum("bhncd,bhnde->bhnce", q_t, s_in)
    return (o_intra + o_inter).reshape(bsz, nh, seq, dv)


def _heads(t, nh):
    bsz, seq, _ = t.shape
    return t.reshape(bsz, seq, nh, -1).transpose(0, 2, 1, 3)


def gla_mixer(h, w_in, w_gate_f, bias_gate_f, w_gate_b, bias_gate_b, out_gain, w_out):
    bsz, seq, _ = h.shape
    proj = h @ w_in
    cuts = np.cumsum([B_QK_WIDTH, B_QK_WIDTH, B_V_WIDTH, B_V_WIDTH, B_GATE_RANK]).tolist()
    q, k, v, r, zf, zb = jnp.split(proj, cuts, axis=-1)
    q = _heads(q, B_HEADS) * (B_KEY_DIM ** -0.5)
    k = _heads(k, B_HEADS)
    v = _heads(v, B_HEADS)
    log_af = jax.nn.log_sigmoid((zf @ w_gate_f + bias_gate_f).astype(jnp.float32)) / B_GATE_TAU
    log_ab = jax.nn.log_sigmoid((zb @ w_gate_b + bias_gate_b).astype(jnp.float32)) / B_GATE_TAU
    log_af = _heads(log_af, B_HEADS)
    log_ab = _heads(log_ab, B_HEADS)
    o_f = gla_chunk(q, k, v, log_af, strict=False)
    flip = lambda t: jnp.flip(t, axis=2)
    o_b = flip(gla_chunk(flip(q), flip(k), flip(v), flip(log_ab), strict=True))
    o = (o_f + o_b).transpose(0, 2, 1, 3)
    o = rms_norm(o, out_gain).reshape(bsz, seq, B_V_WIDTH)
    o = o * jax.nn.silu(r.astype(jnp.float32))
    return o.astype(h.dtype) @ w_out


def swiglu(h, w_gate_up, w_down):
    g, u = jnp.split(h @ w_gate_up, 2, axis=-1)
    return (jax.nn.silu(g) * u) @ w_down


def setup_inputs(seed: int = 0) -> dict:
    key = jax.random.key(seed)
    ks = jax.random.split(key, 16)

    def nrm(k, shape, scale):
        return jax.random.normal(k, shape, jnp.float32) * scale

    return {
        "x": nrm(ks[0], (BATCH, SEQ, D_MODEL), 1.0),
        "attn_norm": 1.0 + nrm(ks[1], (DEPTH, D_MODEL), 0.02),
        "ffn_norm": 1.0 + nrm(ks[2], (DEPTH, D_MODEL), 0.02),
        "a_w_in": nrm(ks[3], (N_A_LAYERS, D_MODEL, A_IN_WIDTH), D_MODEL ** -0.5),
        "a_q_norm": 1.0 + nrm(ks[4], (N_A_LAYERS, A_N_GROUPS, A_HEAD_DIM), 0.02),
        "a_k_norm": 1.0 + nrm(ks[5], (N_A_LAYERS, A_N_GROUPS, A_HEAD_DIM), 0.02),
        "a_w_out": nrm(ks[6], (N_A_LAYERS, A_WIDTH, D_MODEL), A_WIDTH ** -0.5),
        "b_w_in": nrm(ks[7], (N_B_LAYERS, D_MODEL, B_IN_WIDTH), D_MODEL ** -0.5),
        "b_w_gate_f": nrm(ks[8], (N_B_LAYERS, B_GATE_RANK, B_QK_WIDTH), B_GATE_RANK ** -0.5),
        "b_gate_bias_f": nrm(ks[9], (N_B_LAYERS, B_QK_WIDTH), 0.1),
        "b_w_gate_b": nrm(ks[10], (N_B_LAYERS, B_GATE_RANK, B_QK_WIDTH), B_GATE_RANK ** -0.5),
        "b_gate_bias_b": nrm(ks[11], (N_B_LAYERS, B_QK_WIDTH), 0.1),
        "b_out_norm": 1.0 + nrm(ks[12], (N_B_LAYERS, B_HEADS, B_VAL_DIM), 0.02),
        "b_w_out": nrm(ks[13], (N_B_LAYERS, B_V_WIDTH, D_MODEL), B_V_WIDTH ** -0.5),
        "ffn_w_gate_up": nrm(ks[14], (DEPTH, D_MODEL, 2 * FFN_HIDDEN), D_MODEL ** -0.5),
        "ffn_w_down": nrm(ks[15], (DEPTH, FFN_HIDDEN, D_MODEL), FFN_HIDDEN ** -0.5),
    }


def reference(x, attn_norm, ffn_norm, a_w_in, a_q_norm, a_k_norm, a_w_out, b_w_in, b_w_gate_f, b_gate_bias_f, b_w_gate_b, b_gate_bias_b, b_out_norm, b_w_out, ffn_w_gate_up, ffn_w_down):
    positions = jnp.arange(x.shape[1])
    h = x
    for i in range(DEPTH):
        j = i // N_MIXERS
        hn = rms_norm(h, attn_norm[i])
        if i % N_MIXERS == 0:
            mix = dilated_attention_mixer(hn, a_w_in[j], a_q_norm[j], a_k_norm[j], a_w_out[j], positions)
        else:
            mix = gla_mixer(hn, b_w_in[j], b_w_gate_f[j], b_gate_bias_f[j], b_w_gate_b[j], b_gate_bias_b[j], b_out_norm[j], b_w_out[j])
        h = h + mix.astype(h.dtype)
        h = h + swiglu(rms_norm(h, ffn_norm[i]), ffn_w_gate_up[i], ffn_w_down[i]).astype(h.dtype)
    return h
```

```python
import math
from contextlib import ExitStack

import numpy as np
import ml_dtypes

import concourse.bass as bass
import concourse.mybir as mybir
from concourse.bass_utils import run_bass_kernel_spmd

F32 = mybir.dt.float32
BF16 = mybir.dt.bfloat16
ALU = mybir.AluOpType
AF = mybir.ActivationFunctionType

D = 1024
FH = 2816
NJ = FH // 128
EPS = 1e-6
A_GROUPS = ((128, 1), (512, 4), (2048, 16))
CB_B32, CB_M1, CB_M2, CB_ONES, CB_ONESF, CB_ONESL, CB_M1F, CB_M2L, CB_GMF, CB_GMB = 0, 128, 256, 384, 448, 512, 576, 704, 832, 960
NCB = 1088
CF_TRI_LE, CF_TRI_GE, CF_TRI_GT, CF_TRI_LT = 0, 128, 256, 384
CF_EPS64, CF_LN8, CF_ONE = 0, 1, 3
NCF = 580


class T:
    __slots__ = ("ap", "w", "r", "name")

    def __init__(self, ap, name=""):
        self.ap = ap
        self.w = None
        self.r = {}
        self.name = name

    def __getitem__(self, k):
        return self.ap[k]


class Ring:
    def __init__(self, tiles):
        self.tiles = tiles
        self.i = 0

    def next(self):
        t = self.tiles[self.i % len(self.tiles)]
        self.i += 1
        return t


class KB:
    LIMIT = 30000
    NDMA = 24

    def __init__(self, nc, stack):
        self.nc = nc
        self.stack = stack
        self.E = dict(pe=nc.tensor, act=nc.scalar, dve=nc.vector, pool=nc.gpsimd, sp=nc.sync)
        self.sems = []
        self.cur = {}
        self.cnt = {}
        self.known = {e: {} for e in self.E}
        self.nsem = 0
        for e in ("pe", "act", "dve", "pool"):
            self._new_sem(e)
        self.dma_sems = [self._alloc(f"dq{i}") for i in range(self.NDMA)]
        self.dma_val = {k: 0 for k in self.dma_sems}
        self.dma_rr = 0
        self.ninst = {e: 0 for e in self.E}
        self.pend_unsig = {}

    def _alloc(self, name):
        h = self.stack.enter_context(self.nc.semaphore(f"{name}_{self.nsem}"))
        self.nsem += 1
        self.sems.append(h)
        return len(self.sems) - 1

    def _new_sem(self, e):
        self.cur[e] = self._alloc(f"e_{e}")
        self.cnt[e] = 0

    def sb(self, stack, name, shape, dtype):
        self.nsem += 1
        name = f"sb_{name}_{self.nsem}"
        t = stack.enter_context(self.nc.sbuf_tensor(name, list(shape), dtype))
        return T(t[:] if hasattr(t, "__getitem__") else t, name)

    def sb_ring(self, stack, name, shape, dtype, n):
        return Ring([self.sb(stack, f"{name}{i}", shape, dtype) for i in range(n)])

    def ps(self, stack, name, shape, dtype):
        self.nsem += 1
        name = f"ps_{name}_{self.nsem}"
        t = stack.enter_context(self.nc.psum_tensor(name, list(shape), dtype))
        return T(t[:] if hasattr(t, "__getitem__") else t, name)

    def _wait(self, eng, toks):
        need = {}
        kn = self.known[eng]
        for t in toks:
            if t is None:
                continue
            k, v = t
            if kn.get(k, 0) >= v:
                continue
            if need.get(k, 0) < v:
                need[k] = v
        for k, v in need.items():
            self.E[eng].wait_ge(self.sems[k], v)
            kn[k] = v
            self.ninst[eng] += 1

    def _deps(self, eng, reads, writes):
        toks = []
        own = self.cur.get(eng)
        for t in reads:
            w = t.w
            if w is not None and not (eng == "pe" and w[0] == own):
                toks.append(w)
        for t in writes:
            w = t.w
            if w is not None and w[0] != own:
                toks.append(w)
            for e, tok in t.r.items():
                if e != eng:
                    toks.append(tok)
        return toks

    def op(self, eng, fn, reads=(), writes=(), sig=True):
        if self.cnt[eng] >= self.LIMIT and not self.pend_unsig.get(eng):
            self._new_sem(eng)
        self._wait(eng, self._deps(eng, reads, writes))
        ins = fn(self.E[eng])
        self.ninst[eng] += 1
        own = self.cur[eng]
        if sig or eng != "pe":
            self.cnt[eng] += 1
            ins.then_inc(self.sems[own], 1)
            tok = (own, self.cnt[eng])
            self.pend_unsig[eng] = False
        else:
            tok = (own, self.cnt[eng] + 1)
            self.pend_unsig[eng] = True
        for t in reads:
            t.r[eng] = tok
        for t in writes:
            t.w = tok
            t.r = {}
        return tok

    def dma(self, q, out_ap, in_ap, reads=(), writes=(), **kw):
        k = self.dma_sems[self.dma_rr % self.NDMA]
        self.dma_rr += 1
        prev = self.dma_val[k]
        toks = self._deps(("d", k), reads, writes)
        if prev:
            toks.append((k, prev))
        self._wait(q, toks)
        self.E[q].dma_start(out=out_ap, in_=in_ap, **kw).then_inc(self.sems[k], 16)
        self.ninst[q] += 1
        self.dma_val[k] = prev + 16
        tok = (k, prev + 16)
        for t in reads:
            t.r[("d", k)] = tok
        for t in writes:
            t.w = tok
            t.r = {}
        return tok

    def barrier(self):
        assert not self.pend_unsig.get("pe"), "unsignalled PE instruction pending at barrier"
        toks = [(self.cur[e], self.cnt[e]) for e in ("pe", "act", "dve", "pool") if self.cnt[e]]
        toks += [(k, v) for k, v in self.dma_val.items() if v]
        for e in self.E:
            self._wait(e, toks)


class Prog:
    def __init__(self, S, layers, n_ffn_only=False):
        self.S = S
        self.layers = layers

    def build(self):
        S = self.S
        nc = bass.Bass("TRN2", target_bir_lowering=False)
        self.nc = nc
        L = len(self.layers)
        na = sum(1 for t in self.layers if t == "a")
        nb = sum(1 for t in self.layers if t == "b")
        din = lambda name, shape, dt=F32: nc.dram_tensor(name, list(shape), dt, kind="ExternalInput").ap()
        self.x = din("x", [S, D])
        self.attn_norm = din("attn_norm", [L, 128, 8])
        self.ffn_norm = din("ffn_norm", [L, 128, 8])
        self.w_gu = din("ffn_w_gate_up", [L, D, 2 * FH])
        self.w_dn = din("ffn_w_down", [L, FH, D])
        self.ident_in = din("ident", [128, 128], BF16)
        self.consts_bf = din("consts_bf", [128, NCB], BF16)
        self.consts_f = din("consts_f", [128, NCF], F32)
        if na:
            self.a_w_in = din("a_w_in", [na, D, 9216])
            self.a_w_out = din("a_w_out", [na, D, D])
            self.qk_gain = din("qk_gain", [na, 128, 12])
            self.rope_cos = din("rope_cos", [3, 32, S])
            self.rope_sin = din("rope_sin", [3, 32, S])
            dsc = lambda name, shape, dt: nc.dram_tensor(name, list(shape), dt).ap()
            self.qT = [dsc(f"qT{g}", [2, 512, S], BF16) for g in range(3)]
            self.kT = [dsc(f"kT{g}", [2, 512, S], BF16) for g in range(3)]
            self.vv = [dsc(f"vv{g}", [S, D], BF16) for g in range(3)]
            self.oT = dsc("oT", [D, S], BF16)
        if nb:
            self.b_w_in = din("b_w_in", [nb, D, 3104])
            self.b_w_gate = [din("b_w_gate_f", [nb, 16, 512]), din("b_w_gate_b", [nb, 16, 512])]
            self.b_gate_bias = [din("b_gate_bias_f", [nb, 512]), din("b_gate_bias_b", [nb, 512])]
            self.b_out_norm = din("b_out_norm", [nb, D])
            self.b_w_out = din("b_w_out", [nb, D, D])
            dsc = lambda name, shape, dt: nc.dram_tensor(name, list(shape), dt).ap()
            self.g_qt = [dsc(f"g_qt{i}", [512, S], BF16) for i in range(2)]
            self.g_kt = [dsc(f"g_kt{i}", [512, S], BF16) for i in range(2)]
            self.g_kend = [dsc(f"g_kend{i}", [S, 512], BF16) for i in range(2)]
            self.g_v = dsc("g_v", [S, D], BF16)
            self.g_sr = dsc("g_sr", [S, D], BF16)
            self.g_o = [dsc(f"g_o{i}", [S, D], F32) for i in range(2)]
            self.g_yT = dsc("g_yT", [D, S], BF16)
        self.out = nc.dram_tensor("out", [S, D], F32, kind="ExternalOutput").ap()
        self.h = nc.dram_tensor("h_scratch", [S, D], F32).ap()

        with ExitStack() as stack:
            kb = KB(nc, stack)
            self.kb = kb
            self.ident = kb.sb(stack, "ident", [128, 128], BF16)
            kb.dma("sp", self.ident[:, :], self.ident_in[:, :], writes=[self.ident])
            self.gain_a = kb.sb(stack, "gain_a", [128, L, 8], F32)
            self.gain_f = kb.sb(stack, "gain_f", [128, L, 8], F32)
            kb.dma("sp", self.gain_a[:, :, :], self.attn_norm.rearrange("l p k -> p l k"), writes=[self.gain_a])
            kb.dma("sp", self.gain_f[:, :, :], self.ffn_norm.rearrange("l p k -> p l k"), writes=[self.gain_f])
            self.h_tiles = [T(None, f"h{i}") for i in range(S // 128)]
            self.cbf = kb.sb(stack, "cbf", [128, CB_M1F], BF16)
            self.cf = kb.sb(stack, "cf", [128, 4], F32)
            kb.dma("sp", self.cbf[:, :], self.consts_bf[:, 0:CB_M1F], writes=[self.cbf])
            kb.dma("sp", self.cf[:, :], self.consts_f[:, 512:516], writes=[self.cf])

            for li, typ in enumerate(self.layers):
                src = self.x if li == 0 else self.h
                last = li == L - 1
                if typ == "f":
                    h_in = src
                elif typ == "a":
                    ja = sum(1 for t in self.layers[:li] if t == "a")
                    self.attn_layer(li, ja, src)
                    h_in = self.h
                elif typ == "b":
                    jb = sum(1 for t in self.layers[:li] if t == "b")
                    self.gla_layer(li, jb, src)
                    h_in = self.h
                else:
                    raise NotImplementedError
                dst = self.out if last else self.h
                self.ffn_phase(li, h_in, dst)
            kb.barrier()
        return nc

    def load_w_bf16(self, stack_unused, dst_tile, dst_ap_fn, src_rows_ap, ncols, stage_ring, cast_eng="pool", headperm=False):
        kb = self.kb
        CH = 2048
        for c0 in range(0, ncols, CH):
            cw = min(CH, ncols - c0)
            st = stage_ring.next()
            kb.dma("sp", st[:, :cw], src_rows_ap[:, c0:c0 + cw], writes=[st])
            dst_ap = dst_ap_fn(c0, cw)
            src = st[:, :cw]
            pairs = [(dst_ap, src)]
            if headperm:
                assert cw == 1024
                pairs = []
                for half in range(2):
                    i_ap = bass.AP(src.tensor, src.offset + half * 32, [list(src.ap[0]), [256, 4], [64, 4], [1, 32]])
                    o_ap = bass.AP(dst_ap.tensor, dst_ap.offset + half * 128, [list(dst_ap.ap[0]), [256, 4], [32, 4], [1, 32]])
                    pairs.append((o_ap, i_ap))
            for o, i in pairs:
                if cast_eng == "act":
                    kb.op("act", lambda e, o=o, i=i: e.copy(o, i), reads=[st], writes=[dst_tile])
                else:
                    kb.op(cast_eng, lambda e, o=o, i=i: e.tensor_copy(o, i), reads=[st], writes=[dst_tile])

    def norm_T(self, src_dram, tile_idx, gain, li, hring, xsring, psT, hnT, col0, ss_ring, perm=None):
        kb = self.kb
        ht = hring.next()
        r0 = tile_idx * 128
        rd = [self.h_tiles[tile_idx]] if src_dram is self.h else []
        kb.dma("sp", ht[:, :], src_dram[r0:r0 + 128, :], reads=rd, writes=[ht])
        xs = xsring.next()
        ss = ss_ring.next()
        kb.op("dve", lambda e: e.memset(ss[:, 0:1], 0.0), writes=[ss])
        kb.op("act", lambda e: e.activation(xs[:, :], ht[:, :], AF.Square, accum_out=ss[:, 0:1]),
              reads=[ht, ss], writes=[xs, ss])
        kb.op("dve", lambda e: e.tensor_scalar(ss[:, 1:2], ss[:, 0:1], 1.0 / D, EPS, ALU.mult, ALU.add),
              reads=[ss], writes=[ss])
        kb.op("act", lambda e: e.activation(ss[:, 2:3], ss[:, 1:2], AF.Ln), reads=[ss], writes=[ss])
        kb.op("act", lambda e: e.activation(ss[:, 3:4], ss[:, 2:3], AF.Exp, scale=-0.5), reads=[ss], writes=[ss])
        kb.op("act", lambda e: e.activation(xs[:, :], ht[:, :], AF.Copy, scale=ss[:, 3:4]),
              reads=[ht, ss], writes=[xs])
        pt = psT.next()
        ptb = pt.ap.bitcast(BF16)
        for kc in range(8):
            kb.op("pe", lambda e, kc=kc: e.transpose(ptb[:, kc * 128:(kc + 1) * 128],
                                                      xs[:, kc * 128:(kc + 1) * 128], self.ident[:, :]),
                  reads=[xs, self.ident], writes=[pt], sig=(kc == 7))
        g_ap = gain.ap[:, li, :]
        if perm is None or perm[0] == 1:
            if perm is not None:
                col0 = perm[2]
            g_b = bass.AP(g_ap.tensor, g_ap.offset, [list(g_ap.ap[0]), list(g_ap.ap[1]), [0, 128]])
            o_ap = hnT.ap[:, :, col0:col0 + 128]
            i_ap = ptb.rearrange("p (k t) -> p k t", k=8)
        else:
            d, JW, cbase = perm
            nj = 128 // d
            g_b = bass.AP(g_ap.tensor, g_ap.offset, [list(g_ap.ap[0]), list(g_ap.ap[1]), [0, nj], [0, d]])
            ha = hnT.ap
            o_ap = bass.AP(ha.tensor, ha.offset + cbase, [list(ha.ap[0]), list(ha.ap[1]), [1, nj], [JW, d]])
            i_ap = bass.AP(ptb.tensor, ptb.offset, [list(ptb.ap[0]), [128, 8], [d, nj], [1, d]])
        kb.op("dve", lambda e: e.tensor_tensor(o_ap, i_ap, g_b, ALU.mult), reads=[pt, gain], writes=[hnT])
        return ht

    def attn_layer(self, li, ja, src):
        kb, S = self.kb, self.S
        LN8 = math.log(8.0)
        ST = min(2048, S)
        cbf = self.cbf
        for g, (win, d) in enumerate(A_GROUPS):
            Lg = S // d
            JW = ST // d
            with ExitStack() as ph:
                W = [kb.sb(ph, f"aw{t}", [128, 8, D], BF16) for t in range(3)]
                with ExitStack() as ld:
                    stg = kb.sb_ring(ld, "awst", [128, 2048], F32, 3)
                    for t in range(3):
                        c0 = (g * 3 + t) * 1024
                        for kc in range(8):
                            self.load_w_bf16(None, W[t], lambda cc, cw, kc=kc, t=t: W[t].ap[:, kc, cc:cc + cw],
                                             self.a_w_in[ja, kc * 128:(kc + 1) * 128, c0:c0 + 1024], 1024, stg,
                                             cast_eng=("pool" if kc % 2 == 0 else "act"), headperm=(t < 2))
                    kb.barrier()
                hring = kb.sb_ring(ph, "ah", [128, D], F32, 3)
                xsring = kb.sb_ring(ph, "axs", [128, D], BF16, 2)
                ssring = kb.sb_ring(ph, "ass", [128, 4], F32, 4)
                hn = kb.sb(ph, "ahn", [128, 8, ST], BF16)
                cosT = kb.sb(ph, "acos", [128, ST], F32)
                sinT = kb.sb(ph, "asin", [128, ST], F32)
                qkg = kb.sb(ph, "aqkg", [128, 12], F32)
                kb.dma("sp", qkg[:, :], self.qk_gain[ja], writes=[qkg])
                stage = kb.sb_ring(ph, "ast", [128, ST], BF16, 4)
                sqring = kb.sb_ring(ph, "asq", [128, 512], BF16, 6)
                rring = kb.sb_ring(ph, "ar", [128, 512], F32, 2)
                abring = kb.sb_ring(ph, "aab", [128, 512], F32, 6)
                tmpd = kb.sb_ring(ph, "atd", [128, 512], F32, 4)
                tmpp = kb.sb_ring(ph, "atp", [128, 512], F32, 4)
                vst = kb.sb_ring(ph, "avs", [128, D], BF16, 2)
                psT = Ring([kb.ps(ph, "apT", [128, 512], F32)])
                psA = Ring([kb.ps(ph, f"apA{i}", [128, 512], F32) for i in range(6)])
                psB = psA
                psV = psA
                psN = Ring([kb.ps(ph, "apN", [128, 512], F32)])
                for st in range(S // ST):
                    T0 = st * ST
                    for tab, src_t in ((cosT, self.rope_cos), (sinT, self.rope_sin)):
                        for hl in range(4):
                            ta = tab.ap
                            dst = bass.AP(ta.tensor, ta.offset + 32 * hl * ta.ap[0][0],
                                          [[ta.ap[0][0], 32], [JW, d], [1, JW]])
                            sa = src_t[g]
                            sr = bass.AP(sa.tensor, sa.offset + T0 // d, [[S, 32], [Lg, d], [1, JW]])
                            kb.dma("sp", dst, sr, writes=[tab])
                    for sub in range(ST // 128):
                        self.norm_T(src, T0 // 128 + sub, self.gain_a, li, hring, xsring, psT, hn, None, ssring,
                                    perm=(d, JW, sub * (128 // d)))
                    pend = None
                    for t in range(2):
                        Wt = W[t].ap
                        dst_t = self.qT[g] if t == 0 else self.kT[g]
                        for fbp in range(4):
                            stA, stB = stage.next(), stage.next()
                            for tt in range(ST // 512):
                                cols = slice(tt * 512, (tt + 1) * 512)
                                pA, pB = psA.next(), psB.next()
                                for half, ps in ((0, pA), (1, pB)):
                                    for kc in range(8):
                                        lhs = Wt[:, kc, fbp * 256 + half * 128:fbp * 256 + (half + 1) * 128]
                                        kb.op("pe", lambda e, ps=ps, lhs=lhs, kc=kc: e.matmul(
                                            ps[:, :], lhs, hn.ap[:, kc, cols], start=(kc == 0), stop=(kc == 7)),
                                            reads=[W[t], hn], writes=[ps], sig=(kc == 7))
                                sqA, sqB = sqring.next(), sqring.next()
                                kb.op("act", lambda e: e.activation(sqA[:, :], pA[:, :], AF.Square), reads=[pA], writes=[sqA])
                                kb.op("act", lambda e: e.activation(sqB[:, :], pB[:, :], AF.Square), reads=[pB], writes=[sqB])

                                def stage2(t=t, fbp=fbp, tt=tt, cols=cols, pA=pA, pB=pB, sqA=sqA, sqB=sqB, stA=stA, stB=stB, dst_t=dst_t):
                                    pN = psN.next()
                                    kb.op("pe", lambda e: e.matmul(pN[:, :], cbf[:, CB_B32:CB_B32 + 128], sqA[:, :], start=True, stop=False),
                                          reads=[cbf, sqA], writes=[pN])
                                    kb.op("pe", lambda e: e.matmul(pN[:, :], cbf[:, CB_B32:CB_B32 + 128], sqB[:, :], start=False, stop=True),
                                          reads=[cbf, sqB], writes=[pN])
                                    r = rring.next()
                                    kb.op("act", lambda e: e.activation(r[:, :], pN[:, :], AF.Ln, bias=self.cf[:, CF_EPS64:CF_EPS64 + 1]),
                                          reads=[pN, self.cf], writes=[r])
                                    kb.op("act", lambda e: e.activation(r[:, :], r[:, :], AF.Exp, scale=-0.5,
                                                                        bias=self.cf[:, CF_LN8 + (1 - t):CF_LN8 + (1 - t) + 1]),
                                          reads=[r, self.cf], writes=[r])
                                    a_, b_ = abring.next(), abring.next()
                                    gA = qkg[:, g * 4 + t * 2:g * 4 + t * 2 + 1]
                                    gB = qkg[:, g * 4 + t * 2 + 1:g * 4 + t * 2 + 2]
                                    kb.op("dve", lambda e: e.scalar_tensor_tensor(a_[:, :], pA[:, :], gA, r[:, :], ALU.mult, ALU.mult),
                                          reads=[pA, qkg, r], writes=[a_])
                                    kb.op("dve", lambda e: e.scalar_tensor_tensor(b_[:, :], pB[:, :], gB, r[:, :], ALU.mult, ALU.mult),
                                          reads=[pB, qkg, r], writes=[b_])
                                    t1, t2 = tmpd.next(), tmpd.next()
                                    t3, t4 = tmpp.next(), tmpp.next()
                                    kb.op("dve", lambda e: e.tensor_tensor(t1[:, :], a_[:, :], cosT[:, cols], ALU.mult), reads=[a_, cosT], writes=[t1])
                                    kb.op("dve", lambda e: e.tensor_tensor(t2[:, :], b_[:, :], sinT[:, cols], ALU.mult), reads=[b_, sinT], writes=[t2])
                                    kb.op("dve", lambda e: e.tensor_tensor(stA[:, cols], t1[:, :], t2[:, :], ALU.subtract), reads=[t1, t2], writes=[stA])
                                    kb.op("pool", lambda e: e.tensor_tensor(t3[:, :], b_[:, :], cosT[:, cols], ALU.mult), reads=[b_, cosT], writes=[t3])
                                    kb.op("pool", lambda e: e.tensor_tensor(t4[:, :], a_[:, :], sinT[:, cols], ALU.mult), reads=[a_, sinT], writes=[t4])
                                    kb.op("pool", lambda e: e.tensor_tensor(stB[:, cols], t3[:, :], t4[:, :], ALU.add), reads=[t3, t4], writes=[stB])
                                    if tt == ST // 512 - 1:
                                        for half, stx in ((0, stA), (1, stB)):
                                            da = dst_t
                                            dap = bass.AP(da.tensor, da.offset + half * 512 * S + fbp * 128 * S + T0 // d,
                                                          [[S, 128], [Lg, d], [1, JW]])
                                            sx = stx.ap
                                            sap = bass.AP(sx.tensor, sx.offset, [list(sx.ap[0]), [JW, d], [1, JW]])
                                            kb.dma("pool", dap, sap, reads=[stx])

                                if pend is not None:
                                    pend()
                                pend = stage2
                    if pend is not None:
                        pend()
                        pend = None
                    for tsub in range(ST // 128):
                        vs = vst.next()
                        for half in range(2):
                            pv = psV.next()
                            for kc in range(8):
                                kb.op("pe", lambda e, kc=kc: e.matmul(pv[:, :], hn.ap[:, kc, tsub * 128:(tsub + 1) * 128],
                                                                      W[2].ap[:, kc, half * 512:(half + 1) * 512],
                                                                      start=(kc == 0), stop=(kc == 7)),
                                      reads=[hn, W[2]], writes=[pv], sig=(kc == 7))
                            kb.op("act", lambda e: e.copy(vs[:, half * 512:(half + 1) * 512], pv[:, :]), reads=[pv], writes=[vs])
                        p = (tsub * 128) // JW
                        jj0 = (tsub * 128) % JW
                        row0 = p * Lg + T0 // d + jj0
                        kb.dma("pool", self.vv[g][row0:row0 + 128, :], vs[:, :], reads=[vs])
                kb.barrier()

        ST2 = min(2048, S)
        with ExitStack() as ph:
            accR = kb.sb_ring(ph, "acc", [128, ST2], F32, 2)
            amk = kb.sb(ph, "amk", [128, 256], BF16)
            kb.dma("sp", amk[:, :], self.consts_bf[:, CB_M1F:CB_M1F + 256], writes=[amk])
            shf = kb.sb(ph, "ashf", [128, 64], F32)
            kb.dma("sp", shf[:, :], self.consts_f[:, 516:580], writes=[shf])
            kbuf = kb.sb_ring(ph, "akb", [64, 2 * ST2], BF16, 4)
            qbuf = kb.sb_ring(ph, "aqb", [64, ST2], BF16, 4)
            vbuf = kb.sb_ring(ph, "avb", [128, 4096], BF16, 4)
            for t_ in kbuf.tiles:
                kb.op("pool", lambda e, t_=t_: e.memset(t_[:, :], 0.0), writes=[t_])
            for t_ in vbuf.tiles:
                kb.op("pool", lambda e, t_=t_: e.memset(t_[:, :], 0.0), writes=[t_])
                kb.op("pool", lambda e, t_=t_: e.memset(t_.ap.rearrange("p (s c) -> p s c", c=128)[:, :, 64:128], 1.0), writes=[t_])
            Ptr = kb.sb_ring(ph, "aPt", [128, 512], BF16, 8)
            ost = kb.sb_ring(ph, "aos", [64, ST2], BF16, 2)
            rBr = kb.sb_ring(ph, "arB", [64, ST2], F32, 1)
            psS = Ring([kb.ps(ph, f"apS{i}", [128, 512], F32) for i in range(6)])
            psA = Ring([kb.ps(ph, f"apA{i}", [128, 512], F32) for i in range(2)])
            for st in range(S // ST2):
                T0 = st * ST2
                for h in range(16):
                    acc = accR.next()
                    pending = []
                    for g, (win, d) in enumerate(A_GROUPS):
                        Lg = S // d
                        JW2 = ST2 // d
                        NQ = JW2 // 128
                        NKT = NQ + 1
                        KW = JW2 + 128
                        j0 = T0 // d
                        lo_clip = (j0 == 0)
                        hi_clip = (j0 + JW2 == Lg)
                        kt, qt, vt = kbuf.next(), qbuf.next(), vbuf.next()
                        kta, qta, vta = kt.ap, qt.ap, vt.ap
                        kps, vps = kta.ap[0][0], vta.ap[0][0]
                        c_lo = 64 if lo_clip else 0
                        c_hi = KW - 64 if hi_clip else KW
                        for half in range(2):
                            ks = self.kT[g]
                            sap = bass.AP(ks.tensor, ks.offset + half * 512 * S + h * 32 * S + (j0 - 64 + c_lo),
                                          [[S, 32], [Lg, d], [1, c_hi - c_lo]])
                            dap = bass.AP(kta.tensor, kta.offset + 32 * half * kps + c_lo, [[kps, 32], [KW, d], [1, c_hi - c_lo]])
                            kb.dma("sp", dap, sap, writes=[kt])
                            qs = self.qT[g]
                            sap = bass.AP(qs.tensor, qs.offset + half * 512 * S + h * 32 * S + j0, [[S, 32], [Lg, d], [1, JW2]])
                            dap = bass.AP(qta.tensor, qta.offset + 32 * half * qta.ap[0][0], [[qta.ap[0][0], 32], [JW2, d], [1, JW2]])
                            kb.dma("sp", dap, sap, writes=[qt])
                        vs = self.vv[g]
                        i_lo = 1 if lo_clip else 0
                        i_hi = NKT - 1 if hi_clip else NKT
                        if lo_clip:
                            sap = bass.AP(vs.tensor, vs.offset + h * 64, [[D, 64], [Lg * D, d], [1, 64]])
                            dap = bass.AP(vta.tensor, vta.offset + 64 * vps, [[vps, 64], [NKT * 128, d], [1, 64]])
                            kb.dma("sp", dap, sap, writes=[vt])
                        if hi_clip:
                            sap = bass.AP(vs.tensor, vs.offset + (Lg - 64) * D + h * 64, [[D, 64], [Lg * D, d], [1, 64]])
                            dap = bass.AP(vta.tensor, vta.offset + (NKT - 1) * 128, [[vps, 64], [NKT * 128, d], [1, 64]])
                            kb.dma("sp", dap, sap, writes=[vt])
                        if i_hi > i_lo:
                            ni = i_hi - i_lo
                            if ni == 1 or d == 1:
                                dims_s = [[D, 128], [Lg * D, d], [1, 64]] if ni == 1 else [[D, 128], [128 * D, ni], [1, 64]]
                                dims_d = [[vps, 128], [NKT * 128, d], [1, 64]] if ni == 1 else [[vps, 128], [128, ni], [1, 64]]
                                sap = bass.AP(vs.tensor, vs.offset + (j0 - 64 + 128 * i_lo) * D + h * 64, dims_s)
                                dap = bass.AP(vta.tensor, vta.offset + i_lo * 128, dims_d)
                                kb.dma("sp", dap, sap, writes=[vt])
                            else:
                                for p in range(d):
                                    sap = bass.AP(vs.tensor, vs.offset + (p * Lg + j0 - 64 + 128 * i_lo) * D + h * 64,
                                                  [[D, 128], [128 * D, ni], [1, 64]])
                                    dap = bass.AP(vta.tensor, vta.offset + (p * NKT + i_lo) * 128, [[vps, 128], [128, ni], [1, 64]])
                                    kb.dma("sp", dap, sap, writes=[vt])
                        if NQ >= 4:
                            units = [[(p, q0 + k) for k in range(4)] for p in range(d) for q0 in range(0, NQ, 4)]
                        else:
                            assert NQ == 1 and d % 4 == 0
                            units = [[(p0 + k, 0) for k in range(4)] for p0 in range(0, d, 4)]

                        def stage1(unit, kt=kt, qt=qt, KW=KW, JW2=JW2, lo_clip=lo_clip, hi_clip=hi_clip, NQ=NQ):
                            Pts = []
                            for b2 in range(2):
                                ps = psS.next()
                                for ql in range(2):
                                    p, qi = unit[2 * b2 + ql]
                                    for ti in range(2):
                                        kcol = p * KW + (qi + ti) * 128
                                        oc = (2 * ql + ti) * 128
                                        kb.op("pe", lambda e, kcol=kcol, oc=oc, qi=qi, p=p: e.matmul(
                                            ps[:, oc:oc + 128], kt[:, kcol:kcol + 128],
                                            qt[:, p * JW2 + qi * 128:p * JW2 + (qi + 1) * 128], start=True, stop=True),
                                            reads=[kt, qt], writes=[ps], sig=(ql == 1 and ti == 1))
                                Pt = Ptr.next()
                                kb.op("act", lambda e, Pt=Pt, ps=ps: e.activation(Pt[:, :], ps[:, :], AF.Exp),
                                      reads=[ps], writes=[Pt])
                                ca = cbf.ap
                                clipped = [(lo_clip and unit[2 * b2 + ql][1] == 0, hi_clip and unit[2 * b2 + ql][1] == NQ - 1)
                                           for ql in range(2)]
                                if not any(c_ for pr_ in clipped for c_ in pr_):
                                    msk = bass.AP(ca.tensor, ca.offset + CB_M1, [list(ca.ap[0]), [0, 2], [1, 256]])
                                    ptv = Pt.ap.rearrange("p (a b) -> p a b", a=2)
                                    kb.op("pool", lambda e, ptv=ptv, msk=msk: e.tensor_tensor(ptv, ptv, msk, ALU.mult),
                                          reads=[Pt, cbf], writes=[Pt])
                                else:
                                    for ql in range(2):
                                        for ti in range(2):
                                            mt, mc = cbf, (CB_M1, CB_M2)[ti]
                                            if ti == 0 and clipped[ql][0]:
                                                mt, mc = amk, 0
                                            if ti == 1 and clipped[ql][1]:
                                                mt, mc = amk, 128
                                            oc = (2 * ql + ti) * 128
                                            kb.op("pool", lambda e, oc=oc, mc=mc, mt=mt, Pt=Pt: e.tensor_tensor(
                                                Pt[:, oc:oc + 128], Pt[:, oc:oc + 128], mt[:, mc:mc + 128], ALU.mult),
                                                reads=[Pt, mt], writes=[Pt])
                                Pts.append(Pt)
                            return Pts

                        def stage2(unit, Pts, vt=vt, NKT=NKT, d=d, g=g, acc=acc):
                            pa = psA.next()
                            for ql in range(4):
                                p, qi = unit[ql]
                                Pt = Pts[ql // 2]
                                off = (ql % 2) * 256
                                for ti in range(2):
                                    vcol = (p * NKT + qi + ti) * 128
                                    kb.op("pe", lambda e, vcol=vcol, Pt=Pt, off=off, ti=ti, ql=ql: e.matmul(
                                        pa[:, ql * 128:(ql + 1) * 128], vt[:, vcol:vcol + 128],
                                        Pt[:, off + ti * 128:off + (ti + 1) * 128], start=(ti == 0), stop=(ti == 1)),
                                        reads=[vt, Pt], writes=[pa], sig=(ql == 3 and ti == 1))
                            p0, qi0 = unit[0]
                            aa = acc.ap
                            if unit[1][0] == p0:
                                oap = bass.AP(aa.tensor, aa.offset + (qi0 * 128) * d + p0, [list(aa.ap[0]), [d, 512]])
                                iap = pa[:, :]
                            else:
                                oap = bass.AP(aa.tensor, aa.offset + (qi0 * 128) * d + p0, [list(aa.ap[0]), [1, 4], [d, 128]])
                                iap = pa.ap.rearrange("p (a b) -> p a b", a=4)
                            if g == 0:
                                kb.op("act", lambda e, oap=oap, iap=iap: e.copy(oap, iap), reads=[pa], writes=[acc])
                            else:
                                kb.op("dve", lambda e, oap=oap, iap=iap: e.tensor_tensor(oap, iap, oap, ALU.add),
                                      reads=[pa, acc], writes=[acc])

                        for unit in units:
                            Pts = stage1(unit)
                            pending.append((stage2, unit, Pts))
                            if len(pending) > 2:
                                f_, u_, p_ = pending.pop(0)
                                f_(u_, p_)
                    while pending:
                        f_, u_, p_ = pending.pop(0)
                        f_(u_, p_)
                    rb, ot = rBr.next(), ost.next()
                    for c4 in range(ST2 // 512):
                        c4s = slice(c4 * 512, (c4 + 1) * 512)
                        pbm = psS.next()
                        kb.op("pe", lambda e: e.matmul(pbm[0:64, :], shf[:, :], acc[:, c4s], start=True, stop=True),
                              reads=[shf, acc], writes=[pbm])
                        kb.op("act", lambda e: e.activation(rb[:, c4s], pbm[0:64, :], AF.Ln), reads=[pbm], writes=[rb])
                        kb.op("act", lambda e: e.activation(rb[:, c4s], rb[:, c4s], AF.Exp, scale=-1.0), reads=[rb], writes=[rb])
                        kb.op("dve", lambda e: e.tensor_tensor(ot[:, c4s], acc[0:64, c4s], rb[:, c4s], ALU.mult), reads=[acc, rb], writes=[ot])
                    kb.dma("pool", self.oT[h * 64:(h + 1) * 64, T0:T0 + ST2], ot[:, :], reads=[ot])
            kb.barrier()

        with ExitStack() as ph:
            Wo = kb.sb(ph, "awo", [128, 8, D], BF16)
            with ExitStack() as ld:
                stg = kb.sb_ring(ld, "awst", [128, 2048], F32, 3)
                for kc in range(8):
                    self.load_w_bf16(None, Wo, lambda cc, cw, kc=kc: Wo.ap[:, kc, cc:cc + cw],
                                     self.a_w_out[ja, kc * 128:(kc + 1) * 128, :], D, stg,
                                     cast_eng=("pool" if kc % 2 == 0 else "act"))
                kb.barrier()
            self.out_proj(ph, Wo, self.oT, src)
            kb.barrier()

    def out_proj(self, ph, Wo, yT_dram, src):
        kb, S = self.kb, self.S
        otr = kb.sb_ring(ph, "opy", [128, 8, 512], BF16, 2)
        hring = kb.sb_ring(ph, "oph", [128, D], F32, 4)
        psO = Ring([kb.ps(ph, f"opp{i}", [128, 512], F32) for i in range(4)])
        for tt in range(S // 512):
            ott = otr.next()
            kb.dma("sp", ott[:, :, :], yT_dram[:, tt * 512:(tt + 1) * 512].rearrange("(f p) t -> p f t", p=128), writes=[ott])
            for sub in range(4):
                ti = tt * 4 + sub
                ht = hring.next()
                rd = [self.h_tiles[ti]] if src is self.h else []
                kb.dma("sp", ht[:, :], src[ti * 128:(ti + 1) * 128, :], reads=rd, writes=[ht])
                for half in range(2):
                    po = psO.next()
                    for fc in range(8):
                        kb.op("pe", lambda e, fc=fc: e.matmul(po[:, :], ott.ap[:, fc, sub * 128:(sub + 1) * 128],
                                                              Wo.ap[:, fc, half * 512:(half + 1) * 512],
                                                              start=(fc == 0), stop=(fc == 7)),
                              reads=[ott, Wo], writes=[po], sig=(fc == 7))
                    kb.op("dve", lambda e: e.tensor_tensor(ht[:, half * 512:(half + 1) * 512],
                                                           ht[:, half * 512:(half + 1) * 512], po[:, :], ALU.add),
                          reads=[ht, po], writes=[ht])
                kb.dma("pool", self.h[ti * 128:(ti + 1) * 128, :], ht[:, :], reads=[ht], writes=[self.h_tiles[ti]])

    def gla_layer(self, li, jb, src):
        kb, S = self.kb, self.S
        cbf, cf = self.cbf, self.cf
        NCH = S // 128
        QS = 128 ** -0.5
        with ExitStack() as lay:
            dec = [kb.sb(lay, f"gdec{dr}", [128, 4, NCH], F32) for dr in range(2)]
            tri = kb.sb(lay, "gtri", [128, 512], F32)
            kb.dma("sp", tri[:, :], self.consts_f[:, 0:512], writes=[tri])
            gmsk = kb.sb(lay, "gmsk", [128, 256], BF16)
            kb.dma("sp", gmsk[:, :], self.consts_bf[:, CB_GMF:CB_GMF + 256], writes=[gmsk])
            with ExitStack() as ph:
                Wb = kb.sb(ph, "gw", [128, 8, 3104], BF16)
                wg = [kb.sb(ph, f"gwg{dr}", [33, 512], BF16) for dr in range(2)]
                with ExitStack() as ld:
                    stg = kb.sb_ring(ld, "gwst", [128, 2048], F32, 3)
                    for kc in range(8):
                        self.load_w_bf16(None, Wb, lambda cc, cw, kc=kc: Wb.ap[:, kc, cc:cc + cw],
                                         self.b_w_in[jb, kc * 128:(kc + 1) * 128, :], 3104, stg,
                                         cast_eng=("pool" if kc % 2 == 0 else "act"))
                    gst = kb.sb_ring(ld, "ggst", [33, 512], F32, 2)
                    for dr in range(2):
                        g_ = gst.next()
                        kb.op("pool", lambda e: e.memset(g_[:, :], 0.0), writes=[g_])
                        kb.dma("sp", g_[0:16, :], self.b_w_gate[dr][jb], writes=[g_])
                        kb.dma("sp", g_[32:33, :], self.b_gate_bias[dr][jb:jb + 1, :], writes=[g_])
                        kb.op("pool", lambda e: e.tensor_copy(wg[dr][:, :], g_[:, :]), reads=[g_], writes=[wg[dr]])
                    kb.barrier()
                hring = kb.sb_ring(ph, "gh", [128, D], F32, 3)
                xsring = kb.sb_ring(ph, "gxs", [128, D], BF16, 2)
                ssring = kb.sb_ring(ph, "gss", [128, 4], F32, 4)
                hnring = kb.sb_ring(ph, "ghn", [128, 8, 512], BF16, 2)
                zT = [kb.sb_ring(ph, f"gz{dr}", [33, 512], BF16, 2) for dr in range(2)]
                for dr in range(2):
                    for z in zT[dr].tiles:
                        kb.op("pool", lambda e, z=z: e.memset(z[:, :], 0.0), writes=[z])
                        kb.op("pool", lambda e, z=z: e.memset(z[32:33, :], 1.0), writes=[z])
                ltr = kb.sb_ring(ph, "gl", [128, 512], F32, 4)
                ET = [[kb.sb(ph, f"gE{dr}{sg}", [128, 4, 512], F32) for sg in range(2)] for dr in range(2)]
                Egr = kb.sb_ring(ph, "gEg", [128, 512], F32, 3)
                oqr = kb.sb_ring(ph, "goq", [128, 512], BF16, 4)
                ker = kb.sb_ring(ph, "gke", [128, 512], BF16, 3)
                vor = kb.sb_ring(ph, "gvo", [128, D], BF16, 2)
                sro = kb.sb_ring(ph, "gsr", [128, D], BF16, 2)
                tmr = kb.sb_ring(ph, "gtm", [128, 512], F32, 2)
                psT = Ring([kb.ps(ph, "gpT", [128, 512], F32)])
                psP = Ring([kb.ps(ph, f"gpP{i}", [128, 512], F32) for i in range(3)])
                psC = Ring([kb.ps(ph, f"gpC{i}", [128, 512], F32) for i in range(3)])
                psZ = Ring([kb.ps(ph, "gpZ", [128, 512], F32)])
                tri_cum = (CF_TRI_LE, CF_TRI_GE)
                tri_end = (CF_TRI_GT, CF_TRI_LT)
                for tt in range(S // 512):
                    hn = hnring.next()
                    for sub in range(4):
                        self.norm_T(src, tt * 4 + sub, self.gain_a, li, hring, xsring, psT, hn, sub * 128, ssring)
                    zt = [zT[0].next(), zT[1].next()]
                    for dr in range(2):
                        pz = psZ.next()
                        for kc in range(8):
                            kb.op("pe", lambda e, kc=kc: e.matmul(pz[0:16, :], Wb.ap[:, kc, 3072 + 16 * dr:3088 + 16 * dr],
                                                                  hn.ap[:, kc, :], start=(kc == 0), stop=(kc == 7)),
                                  reads=[Wb, hn], writes=[pz], sig=(kc == 7))
                        kb.op("act", lambda e: e.copy(zt[dr][0:16, :], pz[0:16, :]), reads=[pz], writes=[zt[dr]])
                    for sub in range(4):
                        cs = slice(sub * 128, (sub + 1) * 128)
                        chunk = tt * 4 + sub
                        r0 = chunk * 128
                        lts = []
                        for dr in range(2):
                            pzz = psC.next()
                            kb.op("pe", lambda e: e.matmul(pzz[:, :], zt[dr][0:33, cs], wg[dr][0:33, :], start=True, stop=True),
                                  reads=[zt[dr], wg[dr]], writes=[pzz])
                            lt = ltr.next()
                            kb.op("act", lambda e: e.activation(lt[:, :], pzz[:, :], AF.Exp, scale=-1.0), reads=[pzz], writes=[lt])
                            kb.op("act", lambda e: e.activation(lt[:, :], lt[:, :], AF.Ln, bias=cf[:, CF_ONE:CF_ONE + 1]),
                                  reads=[lt, cf], writes=[lt])
                            lts.append(lt)
                        egs = []
                        for dr in range(2):
                            lt = lts[dr]
                            pc = psC.next()
                            for hq in range(4):
                                kb.op("pe", lambda e, hq=hq: e.matmul(pc[:, hq * 128:(hq + 1) * 128], lt[:, hq * 128:(hq + 1) * 128],
                                                                      tri[:, tri_cum[dr]:tri_cum[dr] + 128], start=True, stop=True),
                                      reads=[lt, tri], writes=[pc])
                            pcv = pc.ap.rearrange("p (h c) -> p h c", h=4)
                            kb.op("act", lambda e: e.activation(ET[dr][0].ap[:, :, cs], pcv, AF.Exp), reads=[pc], writes=[ET[dr][0]])
                            kb.op("act", lambda e: e.activation(ET[dr][1].ap[:, :, cs], pcv, AF.Exp, scale=-1.0), reads=[pc], writes=[ET[dr][1]])
                            ecol = sub * 128 + (127 if dr == 0 else 0)
                            kb.op("dve", lambda e: e.tensor_copy(dec[dr].ap[:, :, chunk], ET[dr][0].ap[:, :, ecol]),
                                  reads=[ET[dr][0]], writes=[dec[dr]])
                            pg = psC.next()
                            kb.op("pe", lambda e: e.matmul(pg[:, :], tri[:, tri_end[dr]:tri_end[dr] + 128], lt[:, :], start=True, stop=True),
                                  reads=[tri, lt], writes=[pg])
                            eg = Egr.next()
                            kb.op("act", lambda e: e.activation(eg[:, :], pg[:, :], AF.Exp), reads=[pg], writes=[eg])
                            egs.append(eg)
                        pk = psP.next()
                        for kc in range(8):
                            kb.op("pe", lambda e, kc=kc: e.matmul(pk[:, :], hn.ap[:, kc, cs], Wb.ap[:, kc, 512:1024],
                                                                  start=(kc == 0), stop=(kc == 7)), reads=[hn, Wb], writes=[pk], sig=(kc == 7))
                        for dr in range(2):
                            ke = ker.next()
                            kb.op("dve", lambda e: e.tensor_tensor(ke[:, :], pk[:, :], egs[dr][:, :], ALU.mult),
                                  reads=[pk, egs[dr]], writes=[ke])
                            kb.dma("pool", self.g_kend[dr][r0:r0 + 128, :], ke[:, :], reads=[ke])
                        vo = vor.next()
                        for half in range(2):
                            pv = psP.next()
                            for kc in range(8):
                                kb.op("pe", lambda e, kc=kc: e.matmul(pv[:, :], hn.ap[:, kc, cs],
                                                                      Wb.ap[:, kc, 1024 + half * 512:1536 + half * 512],
                                                                      start=(kc == 0), stop=(kc == 7)), reads=[hn, Wb], writes=[pv], sig=(kc == 7))
                            kb.op("act", lambda e: e.copy(vo[:, half * 512:(half + 1) * 512], pv[:, :]), reads=[pv], writes=[vo])
                        kb.dma("pool", self.g_v[r0:r0 + 128, :], vo[:, :], reads=[vo])
                        so = sro.next()
                        for half in range(2):
                            pr = psP.next()
                            for kc in range(8):
                                kb.op("pe", lambda e, kc=kc: e.matmul(pr[:, :], hn.ap[:, kc, cs],
                                                                      Wb.ap[:, kc, 2048 + half * 512:2560 + half * 512],
                                                                      start=(kc == 0), stop=(kc == 7)), reads=[hn, Wb], writes=[pr], sig=(kc == 7))
                            tm = tmr.next()
                            kb.op("act", lambda e: e.activation(tm[:, :], pr[:, :], AF.Exp, scale=-1.0), reads=[pr], writes=[tm])
                            kb.op("act", lambda e: e.activation(tm[:, :], tm[:, :], AF.Ln, bias=cf[:, CF_ONE:CF_ONE + 1]),
                                  reads=[tm, cf], writes=[tm])
                            kb.op("act", lambda e: e.activation(tm[:, :], tm[:, :], AF.Exp, scale=-1.0), reads=[tm], writes=[tm])
                            kb.op("dve", lambda e: e.tensor_tensor(so[:, half * 512:(half + 1) * 512], tm[:, :], pr[:, :], ALU.mult),
                                  reads=[tm, pr], writes=[so])
                        kb.dma("pool", self.g_sr[r0:r0 + 128, :], so[:, :], reads=[so])
                    for kind in range(2):
                        for hq in range(4):
                            pq = psP.next()
                            c0 = kind * 512 + hq * 128
                            for kc in range(8):
                                kb.op("pe", lambda e, kc=kc: e.matmul(pq[:, :], Wb.ap[:, kc, c0:c0 + 128], hn.ap[:, kc, :],
                                                                      start=(kc == 0), stop=(kc == 7)), reads=[Wb, hn], writes=[pq], sig=(kc == 7))
                            for dr in range(2):
                                E = ET[dr][kind]
                                oq = oqr.next()
                                kb.op("dve", lambda e: e.scalar_tensor_tensor(oq[:, :], pq[:, :], (QS if kind == 0 else 1.0),
                                                                              E.ap[:, hq, :], ALU.mult, ALU.mult),
                                      reads=[pq, E], writes=[oq])
                                dst = (self.g_qt if kind == 0 else self.g_kt)[dr]
                                kb.dma("pool", dst[hq * 128:(hq + 1) * 128, tt * 512:(tt + 1) * 512], oq[:, :], reads=[oq])
                kb.barrier()

            with ExitStack() as ph:
                Sst = [[kb.sb(ph, f"gS{dr}{hq}", [128, 256], F32) for hq in range(4)] for dr in range(2)]
                Sbf = [[kb.sb(ph, f"gSb{dr}{hq}", [128, 256], BF16) for hq in range(4)] for dr in range(2)]
                for dr in range(2):
                    for hq in range(4):
                        kb.op("pool", lambda e: e.memset(Sst[dr][hq][:, :], 0.0), writes=[Sst[dr][hq]])
                        kb.op("pool", lambda e: e.memset(Sbf[dr][hq][:, :], 0.0), writes=[Sbf[dr][hq]])
                qbr = [kb.sb_ring(ph, f"gqb{dr}", [128, 4, 512], BF16, 2) for dr in range(2)]
                kbr = [kb.sb_ring(ph, f"gkb{dr}", [128, 4, 512], BF16, 2) for dr in range(2)]
                ker = [kb.sb_ring(ph, f"gkeb{dr}", [128, 4, 512], BF16, 2) for dr in range(2)]
                vbr = [kb.sb_ring(ph, f"gvb{dr}", [128, 4, D], BF16, 2) for dr in range(2)]
                amr = kb.sb_ring(ph, "gam", [128, 512], BF16, 4)
                ostr = kb.sb_ring(ph, "gos", [128, D], F32, 4)
                psAt = Ring([kb.ps(ph, f"gpA{i}", [128, 512], F32) for i in range(3)])
                psO = Ring([kb.ps(ph, f"gpO{i}", [128, 512], F32) for i in range(3)])
                psKV = Ring([kb.ps(ph, f"gpK{i}", [128, 512], F32) for i in range(2)])
                NB = S // 512
                pendg = []
                for nb in range(NB):
                    blkt = []
                    for dr in range(2):
                        blk = nb if dr == 0 else NB - 1 - nb
                        ts = slice(blk * 512, (blk + 1) * 512)
                        qb_, kb_, ke_, vb_ = qbr[dr].next(), kbr[dr].next(), ker[dr].next(), vbr[dr].next()
                        kb.dma("sp", qb_[:, :, :], self.g_qt[dr][:, ts].rearrange("(h p) t -> p h t", p=128), writes=[qb_])
                        kb.dma("sp", kb_[:, :, :], self.g_kt[dr][:, ts].rearrange("(h p) t -> p h t", p=128), writes=[kb_])
                        kb.dma("sp", ke_[:, :, :], self.g_kend[dr][ts, :].rearrange("(c p) f -> p c f", p=128), writes=[ke_])
                        kb.dma("sp", vb_[:, :, :], self.g_v[ts, :].rearrange("(c p) f -> p c f", p=128), writes=[vb_])
                        blkt.append((blk, qb_, kb_, ke_, vb_))
                    for ci in range(4):
                        for dr in range(2):
                            blk, qb_, kb_, ke_, vb_ = blkt[dr]
                            c = ci if dr == 0 else 3 - ci
                            cs = slice(c * 128, (c + 1) * 128)
                            gch = blk * 4 + c
                            mcol = 0 if dr == 0 else 128
                            pat = psAt.next()
                            for hq in range(4):
                                kb.op("pe", lambda e, hq=hq: e.matmul(pat[:, hq * 128:(hq + 1) * 128], kb_.ap[:, hq, cs],
                                                                      qb_.ap[:, hq, cs], start=True, stop=True),
                                      reads=[kb_, qb_], writes=[pat])
                            am = amr.next()
                            ca = gmsk.ap
                            msk = bass.AP(ca.tensor, ca.offset + mcol, [list(ca.ap[0]), [0, 4], [1, 128]])
                            kb.op("dve", lambda e, am=am, pat=pat, msk=msk: e.tensor_tensor(
                                am.ap.rearrange("p (a b) -> p a b", a=4), pat.ap.rearrange("p (a b) -> p a b", a=4), msk, ALU.mult),
                                reads=[pat, gmsk], writes=[am])

                            def stageB(dr=dr, qb_=qb_, ke_=ke_, vb_=vb_, c=c, cs=cs, gch=gch, am=am):
                                ot = ostr.next()
                                for hq in range(4):
                                    vs_ = vb_.ap[:, c, hq * 256:(hq + 1) * 256]
                                    po = psO.next()
                                    kb.op("pe", lambda e: e.matmul(po[:, 0:256], am[:, hq * 128:(hq + 1) * 128], vs_, start=True, stop=False),
                                          reads=[am, vb_], writes=[po])
                                    kb.op("pe", lambda e: e.matmul(po[:, 0:256], qb_.ap[:, hq, cs], Sbf[dr][hq][:, :], start=False, stop=True),
                                          reads=[qb_, Sbf[dr][hq]], writes=[po])
                                    pkv = psKV.next()
                                    kb.op("pe", lambda e: e.matmul(pkv[:, 0:256], ke_.ap[:, c, hq * 128:(hq + 1) * 128], vs_, start=True, stop=True),
                                          reads=[ke_, vb_], writes=[pkv])
                                    St = Sst[dr][hq]
                                    kb.op("dve", lambda e: e.scalar_tensor_tensor(St[:, :], St[:, :], dec[dr].ap[:, hq, gch:gch + 1],
                                                                                  pkv[:, 0:256], ALU.mult, ALU.add),
                                          reads=[St, dec[dr], pkv], writes=[St])
                                    kb.op("act", lambda e: e.copy(Sbf[dr][hq][:, :], St[:, :]), reads=[St], writes=[Sbf[dr][hq]])
                                    kb.op("act", lambda e: e.copy(ot[:, hq * 256:(hq + 1) * 256], po[:, 0:256]), reads=[po], writes=[ot])
                                kb.dma("pool", self.g_o[dr][gch * 128:(gch + 1) * 128, :], ot[:, :], reads=[ot])

                            pendg.append(stageB)
                            if len(pendg) > 2:
                                pendg.pop(0)()
                while pendg:
                    pendg.pop(0)()
                kb.barrier()

        with ExitStack() as ph:
            gbc = kb.sb(ph, "ggb", [128, D], F32)
            ga = self.b_out_norm[jb]
            kb.dma("sp", gbc[:, :], bass.AP(ga.tensor, ga.offset, [[0, 128], [1, D]]), writes=[gbc])
            ofr = kb.sb_ring(ph, "gof", [128, D], F32, 2)
            obr = kb.sb_ring(ph, "gob", [128, D], F32, 2)
            srr = kb.sb_ring(ph, "gsi", [128, D], BF16, 2)
            sqr = kb.sb_ring(ph, "gsq", [128, 256], BF16, 2)
            ssr = kb.sb_ring(ph, "gs4", [128, 16], F32, 3)
            ybr = kb.sb_ring(ph, "gyb", [128, D], BF16, 2)
            ytr = kb.sb_ring(ph, "gyt", [128, 8, 128], BF16, 3)
            psT = Ring([kb.ps(ph, f"gpT{i}", [128, 512], F32) for i in range(2)])
            for ti in range(S // 128):
                r0 = ti * 128
                of, ob, sr = ofr.next(), obr.next(), srr.next()
                kb.dma("sp", of[:, :], self.g_o[0][r0:r0 + 128, :], writes=[of])
                kb.dma("sp", ob[:, :], self.g_o[1][r0:r0 + 128, :], writes=[ob])
                kb.dma("sp", sr[:, :], self.g_sr[r0:r0 + 128, :], writes=[sr])
                kb.op("pool", lambda e: e.tensor_tensor(of[:, :], of[:, :], ob[:, :], ALU.add), reads=[of, ob], writes=[of])
                ss = ssr.next()
                kb.op("dve", lambda e: e.memset(ss[:, :], 0.0), writes=[ss])
                for hq in range(4):
                    sq = sqr.next()
                    kb.op("act", lambda e, hq=hq, sq=sq: e.activation(sq[:, :], of[:, hq * 256:(hq + 1) * 256], AF.Square,
                                                                      accum_out=ss[:, hq:hq + 1]), reads=[of, ss], writes=[sq, ss])
                kb.op("dve", lambda e: e.tensor_scalar(ss[:, 4:8], ss[:, 0:4], 1.0 / 256, EPS, ALU.mult, ALU.add), reads=[ss], writes=[ss])
                kb.op("act", lambda e: e.activation(ss[:, 8:12], ss[:, 4:8], AF.Ln), reads=[ss], writes=[ss])
                kb.op("act", lambda e: e.activation(ss[:, 12:16], ss[:, 8:12], AF.Exp, scale=-0.5), reads=[ss], writes=[ss])
                for hq in range(4):
                    kb.op("act", lambda e, hq=hq: e.activation(ob[:, hq * 256:(hq + 1) * 256], of[:, hq * 256:(hq + 1) * 256], AF.Copy,
                                                               scale=ss[:, 12 + hq:13 + hq]), reads=[of, ss], writes=[ob])
                kb.op("dve", lambda e: e.tensor_tensor(ob[:, :], ob[:, :], gbc[:, :], ALU.mult), reads=[ob, gbc], writes=[ob])
                yb = ybr.next()
                kb.op("dve", lambda e: e.tensor_tensor(yb[:, :], ob[:, :], sr[:, :], ALU.mult), reads=[ob, sr], writes=[yb])
                pt = psT.next()
                ptb = pt.ap.bitcast(BF16)
                for kc in range(8):
                    kb.op("pe", lambda e, kc=kc: e.transpose(ptb[:, kc * 128:(kc + 1) * 128], yb[:, kc * 128:(kc + 1) * 128],
                                                              self.ident[:, :]), reads=[yb, self.ident], writes=[pt], sig=(kc == 7))
                yt = ytr.next()
                kb.op("act", lambda e: e.copy(yt.ap.rearrange("p f t -> p (f t)"), ptb), reads=[pt], writes=[yt])
                kb.dma("pool", self.g_yT[:, r0:r0 + 128].rearrange("(f p) t -> p f t", p=128), yt[:, :, :], reads=[yt])
            kb.barrier()
        with ExitStack() as ph:
            Wo = kb.sb(ph, "gwo", [128, 8, D], BF16)
            with ExitStack() as ld:
                stg = kb.sb_ring(ld, "gwst2", [128, 2048], F32, 3)
                for kc in range(8):
                    self.load_w_bf16(None, Wo, lambda cc, cw, kc=kc: Wo.ap[:, kc, cc:cc + cw],
                                     self.b_w_out[jb, kc * 128:(kc + 1) * 128, :], D, stg,
                                     cast_eng=("pool" if kc % 2 == 0 else "act"))
                kb.barrier()
            self.out_proj(ph, Wo, self.g_yT, src)
            kb.barrier()

    def ffn_phase(self, li, src, dst):
        kb, nc, S = self.kb, self.nc, self.S
        TT = 512
        with ExitStack() as ph:
            wgu = kb.sb(ph, "wgu", [128, 8, 2 * FH], BF16)
            wdn = kb.sb(ph, "wdn", [128, NJ, D], BF16)
            with ExitStack() as ld:
                stg = kb.sb_ring(ld, "wstg", [128, 2048], F32, 3)
                for kc in range(8):
                    self.load_w_bf16(None, wgu, lambda c0, cw, kc=kc: wgu.ap[:, kc, c0:c0 + cw],
                                     self.w_gu[li, kc * 128:(kc + 1) * 128, :], 2 * FH, stg,
                                     cast_eng=("pool" if kc % 2 == 0 else "act"))
                for j in range(NJ):
                    self.load_w_bf16(None, wdn, lambda c0, cw, j=j: wdn.ap[:, j, c0:c0 + cw],
                                     self.w_dn[li, j * 128:(j + 1) * 128, :], D, stg,
                                     cast_eng=("pool" if j % 2 == 0 else "act"))
                kb.barrier()
            hring = kb.sb_ring(ph, "fh", [128, D], F32, 8)
            xsring = kb.sb_ring(ph, "fxs", [128, D], BF16, 1)
            ssring = kb.sb_ring(ph, "fss", [128, 4], F32, 4)
            hnring = kb.sb_ring(ph, "fhn", [128, 8, TT], BF16, 2)
            actT = kb.sb(ph, "factT", [128, NJ, TT], BF16)
            sgring = kb.sb_ring(ph, "fsg", [128, TT], F32, 1)
            psT = Ring([kb.ps(ph, f"fpT{i}", [128, 512], F32) for i in range(1)])
            psG = Ring([kb.ps(ph, f"fpG{i}", [128, 512], F32) for i in range(2)])
            psU = Ring([kb.ps(ph, f"fpU{i}", [128, 512], F32) for i in range(2)])
            psD = Ring([kb.ps(ph, f"fpD{i}", [128, 512], F32) for i in range(3)])
            def do_norm(tt):
                hn_ = hnring.next()
                hts_ = []
                for sub in range(TT // 128):
                    hts_.append(self.norm_T(src, tt * (TT // 128) + sub, self.gain_f, li, hring, xsring,
                                            psT, hn_, sub * 128, ssring))
                return hn_, hts_

            nxt = do_norm(0)
            for tt in range(S // TT):
                hn, hts = nxt
                for j in range(NJ):
                    pg, pu = psG.next(), psU.next()
                    for kc in range(8):
                        kb.op("pe", lambda e, kc=kc: e.matmul(pg[:, :], wgu.ap[:, kc, j * 128:(j + 1) * 128],
                                                              hn.ap[:, kc, :], start=(kc == 0), stop=(kc == 7)),
                              reads=[wgu, hn], writes=[pg], sig=(kc == 7))
                    for kc in range(8):
                        kb.op("pe", lambda e, kc=kc: e.matmul(pu[:, :], wgu.ap[:, kc, FH + j * 128:FH + (j + 1) * 128],
                                                              hn.ap[:, kc, :], start=(kc == 0), stop=(kc == 7)),
                              reads=[wgu, hn], writes=[pu], sig=(kc == 7))
                    sg = sgring.next()
                    kb.op("act", lambda e: e.activation(sg[:, :], pg[:, :], AF.Exp, scale=-1.0), reads=[pg], writes=[sg])
                    kb.op("act", lambda e: e.activation(sg[:, :], sg[:, :], AF.Ln, bias=self.cf[:, CF_ONE:CF_ONE + 1]),
                          reads=[sg, self.cf], writes=[sg])
                    kb.op("act", lambda e: e.activation(sg[:, :], sg[:, :], AF.Exp, scale=-1.0), reads=[sg], writes=[sg])
                    kb.op("dve", lambda e: e.tensor_tensor(sg[:, :], sg[:, :], pg[:, :], ALU.mult),
                          reads=[sg, pg], writes=[sg])
                    kb.op("dve", lambda e: e.tensor_tensor(actT.ap[:, j, :], sg[:, :], pu[:, :], ALU.mult),
                          reads=[sg, pu], writes=[actT])
                if tt + 1 < S // TT:
                    nxt = do_norm(tt + 1)
                for sub in range(TT // 128):
                    ti = tt * (TT // 128) + sub
                    ot = hts[sub]
                    for half in range(2):
                        pd = psD.next()
                        for j in range(NJ):
                            kb.op("pe", lambda e, j=j: e.matmul(pd[:, :], actT.ap[:, j, sub * 128:(sub + 1) * 128],
                                                                wdn.ap[:, j, half * 512:(half + 1) * 512],
                                                                start=(j == 0), stop=(j == NJ - 1)),
                                  reads=[actT, wdn], writes=[pd], sig=(j == NJ - 1))
                        kb.op("dve", lambda e: e.tensor_tensor(ot[:, half * 512:(half + 1) * 512],
                                                               hts[sub][:, half * 512:(half + 1) * 512], pd[:, :], ALU.add),
                              reads=[hts[sub], pd], writes=[ot])
                    wr = [self.h_tiles[ti]] if dst is self.h else []
                    kb.dma("pool", dst[ti * 128:(ti + 1) * 128, :], ot[:, :], reads=[ot], writes=wr)
            kb.barrier()


def _consts():
    cb = np.zeros((128, NCB), np.float32)
    k = np.arange(128)[:, None]
    m = np.arange(128)[None, :]
    cb[:, CB_B32:CB_B32 + 128] = (k // 32 == m // 32)
    cb[:, CB_M1:CB_M1 + 128] = (k >= m)
    cb[:, CB_M2:CB_M2 + 128] = (k <= m)
    cb[:, CB_ONES:CB_ONES + 64] = 1.0
    cb[64:, CB_ONESF:CB_ONESF + 64] = 1.0
    cb[:64, CB_ONESL:CB_ONESL + 64] = 1.0
    cb[64:, CB_M1F:CB_M1F + 128] = (k >= m)[64:]
    cb[:64, CB_M2L:CB_M2L + 128] = (k <= m)[:64]
    cb[:, CB_GMF:CB_GMF + 128] = (k <= m)
    cb[:, CB_GMB:CB_GMB + 128] = (k > m)
    cf = np.zeros((128, NCF), np.float32)
    w = -1.0 / 16.0
    cf[:, CF_TRI_LE:CF_TRI_LE + 128] = w * (k <= m)
    cf[:, CF_TRI_GE:CF_TRI_GE + 128] = w * (k >= m)
    cf[:, CF_TRI_GT:CF_TRI_GT + 128] = w * (k > m)
    cf[:, CF_TRI_LT:CF_TRI_LT + 128] = w * (k < m)
    cf[:, 512 + CF_EPS64] = 64 * EPS
    cf[:, 512 + CF_LN8] = math.log(8.0)
    cf[:, 512 + CF_ONE] = 1.0
    cf[:, 516:580] = (k == m[:, :64] + 64)
    return cb.astype(ml_dtypes.bfloat16), cf


def _rope_tables(S):
    half = 32
    inv_freq = (10000.0 ** (-np.arange(half, dtype=np.float32) / half)).astype(np.float32)
    cos = np.zeros((3, 32, S), np.float32)
    sin = np.zeros((3, 32, S), np.float32)
    for g, (_, d) in enumerate(A_GROUPS):
        L = S // d
        pos = (np.arange(L, dtype=np.float32)[None, :] * d + np.arange(d, dtype=np.float32)[:, None]).reshape(-1)
        ang = (pos[None, :].astype(np.float32) * inv_freq[:, None]).astype(np.float32)
        cos[g] = np.cos(ang)
        sin[g] = np.sin(ang)
    return cos, sin


def _qk_gain(qn, kn):
    na = qn.shape[0]
    out = np.zeros((na, 128, 12), np.float32)
    for g in range(3):
        for t, v in enumerate((qn, kn)):
            for half in range(2):
                out[:, :, g * 4 + t * 2 + half] = np.tile(v[:, g, half * 32:(half + 1) * 32], (1, 4))
    return out


def _gla_inputs(inp, nb):
    return dict(b_w_in=inp["b_w_in"][:nb], b_w_gate_f=inp["b_w_gate_f"][:nb], b_w_gate_b=inp["b_w_gate_b"][:nb],
                b_gate_bias_f=inp["b_gate_bias_f"][:nb], b_gate_bias_b=inp["b_gate_bias_b"][:nb],
                b_out_norm=np.ascontiguousarray(inp["b_out_norm"][:nb].reshape(nb, D)), b_w_out=inp["b_w_out"][:nb])


def _prep_small(v):
    L = v.shape[0]
    return np.ascontiguousarray(v.reshape(L, 8, 128).transpose(0, 2, 1)).astype(np.float32)


_PROG_CACHE = {}


def kernel(**inputs):
    x = np.asarray(inputs["x"], dtype=np.float32)
    B, S, _ = x.shape
    pat = "abab"
    L = len(pat)
    na, nb = pat.count("a"), pat.count("b")
    key = (S, pat)
    if key not in _PROG_CACHE:
        _PROG_CACHE[key] = Prog(S, list(pat)).build()
    nc = _PROG_CACHE[key]
    f32 = lambda k: np.ascontiguousarray(np.asarray(inputs[k], dtype=np.float32))
    ident = np.eye(128, dtype=np.float32).astype(ml_dtypes.bfloat16)
    cb, cf = _consts()
    cos, sin = _rope_tables(S)
    shared = dict(
        attn_norm=_prep_small(f32("attn_norm")), ffn_norm=_prep_small(f32("ffn_norm")),
        ffn_w_gate_up=f32("ffn_w_gate_up"), ffn_w_down=f32("ffn_w_down"),
        ident=ident, consts_bf=cb, consts_f=cf,
        a_w_in=f32("a_w_in"), a_w_out=f32("a_w_out"),
        qk_gain=_qk_gain(f32("a_q_norm"), f32("a_k_norm")), rope_cos=cos, rope_sin=sin,
    )
    shared.update(_gla_inputs({k: f32(k) for k in ("b_w_in", "b_w_gate_f", "b_w_gate_b", "b_gate_bias_f",
                                                    "b_gate_bias_b", "b_out_norm", "b_w_out")}, nb))
    in_maps = [dict(shared, x=np.ascontiguousarray(x[c])) for c in range(B)]
    res = run_bass_kernel_spmd(nc, in_maps, core_ids=list(range(B)))
    return np.stack([np.asarray(res.results[c]["out"], dtype=np.float32) for c in range(B)], axis=0)
```

```python
import math
from contextlib import ExitStack

import numpy as np
import ml_dtypes

import concourse.bass as bass
import concourse.mybir as mybir
from concourse.bass_utils import run_bass_kernel_spmd

F32 = mybir.dt.float32
BF16 = mybir.dt.bfloat16
ALU = mybir.AluOpType
AF = mybir.ActivationFunctionType

D = 1024
FH = 2816
NJ = FH // 128
EPS = 1e-6
A_GROUPS = ((128, 1), (512, 4), (2048, 16))
CB_B32, CB_M1, CB_M2, CB_ONES, CB_ONESF, CB_ONESL, CB_M1F, CB_M2L, CB_GMF, CB_GMB = 0, 128, 256, 384, 448, 512, 576, 704, 832, 960
NCB = 1088
CF_TRI_LE, CF_TRI_GE, CF_TRI_GT, CF_TRI_LT = 0, 128, 256, 384
CF_EPS64, CF_LN8, CF_ONE = 0, 1, 3
NCF = 580


class T:
    __slots__ = ("ap", "w", "r", "name")

    def __init__(self, ap, name=""):
        self.ap = ap
        self.w = None
        self.r = {}
        self.name = name

    def __getitem__(self, k):
        return self.ap[k]


class Ring:
    def __init__(self, tiles):
        self.tiles = tiles
        self.i = 0

    def next(self):
        t = self.tiles[self.i % len(self.tiles)]
        self.i += 1
        return t


class KB:
    LIMIT = 30000
    NDMA = 24

    def __init__(self, nc, stack):
        self.nc = nc
        self.stack = stack
        self.E = dict(pe=nc.tensor, act=nc.scalar, dve=nc.vector, pool=nc.gpsimd, sp=nc.sync)
        self.sems = []
        self.cur = {}
        self.cnt = {}
        self.known = {e: {} for e in self.E}
        self.nsem = 0
        for e in ("pe", "act", "dve", "pool"):
            self._new_sem(e)
        self.dma_sems = [self._alloc(f"dq{i}") for i in range(self.NDMA)]
        self.dma_val = {k: 0 for k in self.dma_sems}
        self.dma_rr = 0
        self.ninst = {e: 0 for e in self.E}
        self.pend_unsig = {}

    def _alloc(self, name):
        h = self.stack.enter_context(self.nc.semaphore(f"{name}_{self.nsem}"))
        self.nsem += 1
        self.sems.append(h)
        return len(self.sems) - 1

    def _new_sem(self, e):
        self.cur[e] = self._alloc(f"e_{e}")
        self.cnt[e] = 0

    def sb(self, stack, name, shape, dtype):
        self.nsem += 1
        name = f"sb_{name}_{self.nsem}"
        t = stack.enter_context(self.nc.sbuf_tensor(name, list(shape), dtype))
        return T(t[:] if hasattr(t, "__getitem__") else t, name)

    def sb_ring(self, stack, name, shape, dtype, n):
        return Ring([self.sb(stack, f"{name}{i}", shape, dtype) for i in range(n)])

    def ps(self, stack, name, shape, dtype):
        self.nsem += 1
        name = f"ps_{name}_{self.nsem}"
        t = stack.enter_context(self.nc.psum_tensor(name, list(shape), dtype))
        return T(t[:] if hasattr(t, "__getitem__") else t, name)

    def _wait(self, eng, toks):
        need = {}
        kn = self.known[eng]
        for t in toks:
            if t is None:
                continue
            k, v = t
            if kn.get(k, 0) >= v:
                continue
            if need.get(k, 0) < v:
                need[k] = v
        for k, v in need.items():
            self.E[eng].wait_ge(self.sems[k], v)
            kn[k] = v
            self.ninst[eng] += 1

    def _deps(self, eng, reads, writes):
        toks = []
        own = self.cur.get(eng)
        for t in reads:
            w = t.w
            if w is not None and not (eng == "pe" and w[0] == own):
                toks.append(w)
        for t in writes:
            w = t.w
            if w is not None and w[0] != own:
                toks.append(w)
            for e, tok in t.r.items():
                if e != eng:
                    toks.append(tok)
        return toks

    def op(self, eng, fn, reads=(), writes=(), sig=True):
        if self.cnt[eng] >= self.LIMIT and not self.pend_unsig.get(eng):
            self._new_sem(eng)
        self._wait(eng, self._deps(eng, reads, writes))
        ins = fn(self.E[eng])
        self.ninst[eng] += 1
        own = self.cur[eng]
        if sig or eng != "pe":
            self.cnt[eng] += 1
            ins.then_inc(self.sems[own], 1)
            tok = (own, self.cnt[eng])
            self.pend_unsig[eng] = False
        else:
            tok = (own, self.cnt[eng] + 1)
            self.pend_unsig[eng] = True
        for t in reads:
            t.r[eng] = tok
        for t in writes:
            t.w = tok
            t.r = {}
        return tok

    def dma(self, q, out_ap, in_ap, reads=(), writes=(), **kw):
        k = self.dma_sems[self.dma_rr % self.NDMA]
        self.dma_rr += 1
        prev = self.dma_val[k]
        toks = self._deps(("d", k), reads, writes)
        if prev:
            toks.append((k, prev))
        self._wait(q, toks)
        self.E[q].dma_start(out=out_ap, in_=in_ap, **kw).then_inc(self.sems[k], 16)
        self.ninst[q] += 1
        self.dma_val[k] = prev + 16
        tok = (k, prev + 16)
        for t in reads:
            t.r[("d", k)] = tok
        for t in writes:
            t.w = tok
            t.r = {}
        return tok

    def barrier(self):
        assert not self.pend_unsig.get("pe"), "unsignalled PE instruction pending at barrier"
        toks = [(self.cur[e], self.cnt[e]) for e in ("pe", "act", "dve", "pool") if self.cnt[e]]
        toks += [(k, v) for k, v in self.dma_val.items() if v]
        for e in self.E:
            self._wait(e, toks)


class Prog:
    def __init__(self, S, layers, n_ffn_only=False):
        self.S = S
        self.layers = layers

    def build(self):
        S = self.S
        nc = bass.Bass("TRN2", target_bir_lowering=False)
        self.nc = nc
        L = len(self.layers)
        na = sum(1 for t in self.layers if t == "a")
        nb = sum(1 for t in self.layers if t == "b")
        din = lambda name, shape, dt=F32: nc.dram_tensor(name, list(shape), dt, kind="ExternalInput").ap()
        self.x = din("x", [S, D])
        self.attn_norm = din("attn_norm", [L, 128, 8])
        self.ffn_norm = din("ffn_norm", [L, 128, 8])
        self.w_gu = din("ffn_w_gate_up", [L, D, 2 * FH])
        self.w_dn = din("ffn_w_down", [L, FH, D])
        self.ident_in = din("ident", [128, 128], BF16)
        self.consts_bf = din("consts_bf", [128, NCB], BF16)
        self.consts_f = din("consts_f", [128, NCF], F32)
        if na:
            self.a_w_in = din("a_w_in", [na, D, 9216])
            self.a_w_out = din("a_w_out", [na, D, D])
            self.qk_gain = din("qk_gain", [na, 128, 12])
            self.rope_cos = din("rope_cos", [3, 32, S])
            self.rope_sin = din("rope_sin", [3, 32, S])
            dsc = lambda name, shape, dt: nc.dram_tensor(name, list(shape), dt).ap()
            self.qT = [dsc(f"qT{g}", [2, 512, S], BF16) for g in range(3)]
            self.kT = [dsc(f"kT{g}", [2, 512, S], BF16) for g in range(3)]
            self.vv = [dsc(f"vv{g}", [S, D], BF16) for g in range(3)]
            self.oT = dsc("oT", [D, S], BF16)
        if nb:
            self.b_w_in = din("b_w_in", [nb, D, 3104])
            self.b_w_gate = [din("b_w_gate_f", [nb, 16, 512]), din("b_w_gate_b", [nb, 16, 512])]
            self.b_gate_bias = [din("b_gate_bias_f", [nb, 512]), din("b_gate_bias_b", [nb, 512])]
            self.b_out_norm = din("b_out_norm", [nb, D])
            self.b_w_out = din("b_w_out", [nb, D, D])
            dsc = lambda name, shape, dt: nc.dram_tensor(name, list(shape), dt).ap()
            self.g_qt = [dsc(f"g_qt{i}", [512, S], BF16) for i in range(2)]
            self.g_kt = [dsc(f"g_kt{i}", [512, S], BF16) for i in range(2)]
            self.g_kend = [dsc(f"g_kend{i}", [S, 512], BF16) for i in range(2)]
            self.g_v = dsc("g_v", [S, D], BF16)
            self.g_sr = dsc("g_sr", [S, D], BF16)
            self.g_o = [dsc(f"g_o{i}", [S, D], F32) for i in range(2)]
            self.g_yT = dsc("g_yT", [D, S], BF16)
        self.out = nc.dram_tensor("out", [S, D], F32, kind="ExternalOutput").ap()
        self.h = nc.dram_tensor("h_scratch", [S, D], F32).ap()

        with ExitStack() as stack:
            kb = KB(nc, stack)
            self.kb = kb
            self.ident = kb.sb(stack, "ident", [128, 128], BF16)
            kb.dma("sp", self.ident[:, :], self.ident_in[:, :], writes=[self.ident])
            self.gain_a = kb.sb(stack, "gain_a", [128, L, 8], F32)
            self.gain_f = kb.sb(stack, "gain_f", [128, L, 8], F32)
            kb.dma("sp", self.gain_a[:, :, :], self.attn_norm.rearrange("l p k -> p l k"), writes=[self.gain_a])
            kb.dma("sp", self.gain_f[:, :, :], self.ffn_norm.rearrange("l p k -> p l k"), writes=[self.gain_f])
            self.h_tiles = [T(None, f"h{i}") for i in range(S // 128)]
            self.cbf = kb.sb(stack, "cbf", [128, CB_M1F], BF16)
            self.cf = kb.sb(stack, "cf", [128, 4], F32)
            kb.dma("sp", self.cbf[:, :], self.consts_bf[:, 0:CB_M1F], writes=[self.cbf])
            kb.dma("sp", self.cf[:, :], self.consts_f[:, 512:516], writes=[self.cf])

            for li, typ in enumerate(self.layers):
                src = self.x if li == 0 else self.h
                last = li == L - 1
                if typ == "f":
                    h_in = src
                elif typ == "a":
                    ja = sum(1 for t in self.layers[:li] if t == "a")
                    self.attn_layer(li, ja, src)
                    h_in = self.h
                elif typ == "b":
                    jb = sum(1 for t in self.layers[:li] if t == "b")
                    self.gla_layer(li, jb, src)
                    h_in = self.h
                else:
                    raise NotImplementedError
                dst = self.out if last else self.h
                self.ffn_phase(li, h_in, dst)
            kb.barrier()
        return nc

    def load_w_bf16(self, stack_unused, dst_tile, dst_ap_fn, src_rows_ap, ncols, stage_ring, cast_eng="pool", headperm=False):
        kb = self.kb
        CH = 2048
        for c0 in range(0, ncols, CH):
            cw = min(CH, ncols - c0)
            st = stage_ring.next()
            kb.dma("sp", st[:, :cw], src_rows_ap[:, c0:c0 + cw], writes=[st])
            dst_ap = dst_ap_fn(c0, cw)
            src = st[:, :cw]
            pairs = [(dst_ap, src)]
            if headperm:
                assert cw == 1024
                pairs = []
                for half in range(2):
                    i_ap = bass.AP(src.tensor, src.offset + half * 32, [list(src.ap[0]), [256, 4], [64, 4], [1, 32]])
                    o_ap = bass.AP(dst_ap.tensor, dst_ap.offset + half * 128, [list(dst_ap.ap[0]), [256, 4], [32, 4], [1, 32]])
                    pairs.append((o_ap, i_ap))
            for o, i in pairs:
                if cast_eng == "act":
                    kb.op("act", lambda e, o=o, i=i: e.copy(o, i), reads=[st], writes=[dst_tile])
                else:
                    kb.op(cast_eng, lambda e, o=o, i=i: e.tensor_copy(o, i), reads=[st], writes=[dst_tile])

    def norm_T(self, src_dram, tile_idx, gain, li, hring, xsring, psT, hnT, col0, ss_ring, perm=None):
        kb = self.kb
        ht = hring.next()
        r0 = tile_idx * 128
        rd = [self.h_tiles[tile_idx]] if src_dram is self.h else []
        kb.dma("sp", ht[:, :], src_dram[r0:r0 + 128, :], reads=rd, writes=[ht])
        xs = xsring.next()
        ss = ss_ring.next()
        kb.op("dve", lambda e: e.memset(ss[:, 0:1], 0.0), writes=[ss])
        kb.op("act", lambda e: e.activation(xs[:, :], ht[:, :], AF.Square, accum_out=ss[:, 0:1]),
              reads=[ht, ss], writes=[xs, ss])
        kb.op("dve", lambda e: e.tensor_scalar(ss[:, 1:2], ss[:, 0:1], 1.0 / D, EPS, ALU.mult, ALU.add),
              reads=[ss], writes=[ss])
        kb.op("act", lambda e: e.activation(ss[:, 2:3], ss[:, 1:2], AF.Ln), reads=[ss], writes=[ss])
        kb.op("act", lambda e: e.activation(ss[:, 3:4], ss[:, 2:3], AF.Exp, scale=-0.5), reads=[ss], writes=[ss])
        kb.op("act", lambda e: e.activation(xs[:, :], ht[:, :], AF.Copy, scale=ss[:, 3:4]),
              reads=[ht, ss], writes=[xs])
        pt = psT.next()
        ptb = pt.ap.bitcast(BF16)
        for kc in range(8):
            kb.op("pe", lambda e, kc=kc: e.transpose(ptb[:, kc * 128:(kc + 1) * 128],
                                                      xs[:, kc * 128:(kc + 1) * 128], self.ident[:, :]),
                  reads=[xs, self.ident], writes=[pt], sig=(kc == 7))
        g_ap = gain.ap[:, li, :]
        if perm is None or perm[0] == 1:
            if perm is not None:
                col0 = perm[2]
            g_b = bass.AP(g_ap.tensor, g_ap.offset, [list(g_ap.ap[0]), list(g_ap.ap[1]), [0, 128]])
            o_ap = hnT.ap[:, :, col0:col0 + 128]
            i_ap = ptb.rearrange("p (k t) -> p k t", k=8)
        else:
            d, JW, cbase = perm
            nj = 128 // d
            g_b = bass.AP(g_ap.tensor, g_ap.offset, [list(g_ap.ap[0]), list(g_ap.ap[1]), [0, nj], [0, d]])
            ha = hnT.ap
            o_ap = bass.AP(ha.tensor, ha.offset + cbase, [list(ha.ap[0]), list(ha.ap[1]), [1, nj], [JW, d]])
            i_ap = bass.AP(ptb.tensor, ptb.offset, [list(ptb.ap[0]), [128, 8], [d, nj], [1, d]])
        kb.op("dve", lambda e: e.tensor_tensor(o_ap, i_ap, g_b, ALU.mult), reads=[pt, gain], writes=[hnT])
        return ht

    def attn_layer(self, li, ja, src):
        kb, S = self.kb, self.S
        LN8 = math.log(8.0)
        ST = min(2048, S)
        cbf = self.cbf
        for g, (win, d) in enumerate(A_GROUPS):
            Lg = S // d
            JW = ST // d
            with ExitStack() as ph:
                W = [kb.sb(ph, f"aw{t}", [128, 8, D], BF16) for t in range(3)]
                with ExitStack() as ld:
                    stg = kb.sb_ring(ld, "awst", [128, 2048], F32, 3)
                    for t in range(3):
                        c0 = (g * 3 + t) * 1024
                        for kc in range(8):
                            self.load_w_bf16(None, W[t], lambda cc, cw, kc=kc, t=t: W[t].ap[:, kc, cc:cc + cw],
                                             self.a_w_in[ja, kc * 128:(kc + 1) * 128, c0:c0 + 1024], 1024, stg,
                                             cast_eng=("pool" if kc % 2 == 0 else "act"), headperm=(t < 2))
                    kb.barrier()
                hring = kb.sb_ring(ph, "ah", [128, D], F32, 3)
                xsring = kb.sb_ring(ph, "axs", [128, D], BF16, 2)
                ssring = kb.sb_ring(ph, "ass", [128, 4], F32, 4)
                hn = kb.sb(ph, "ahn", [128, 8, ST], BF16)
                cosT = kb.sb(ph, "acos", [128, ST], F32)
                sinT = kb.sb(ph, "asin", [128, ST], F32)
                qkg = kb.sb(ph, "aqkg", [128, 12], F32)
                kb.dma("sp", qkg[:, :], self.qk_gain[ja], writes=[qkg])
                stage = kb.sb_ring(ph, "ast", [128, ST], BF16, 4)
                sqring = kb.sb_ring(ph, "asq", [128, 512], BF16, 6)
                rring = kb.sb_ring(ph, "ar", [128, 512], F32, 2)
                abring = kb.sb_ring(ph, "aab", [128, 512], F32, 6)
                tmpd = kb.sb_ring(ph, "atd", [128, 512], F32, 4)
                tmpp = kb.sb_ring(ph, "atp", [128, 512], F32, 4)
                vst = kb.sb_ring(ph, "avs", [128, D], BF16, 2)
                psT = Ring([kb.ps(ph, "apT", [128, 512], F32)])
                psA = Ring([kb.ps(ph, f"apA{i}", [128, 512], F32) for i in range(6)])
                psB = psA
                psV = psA
                psN = Ring([kb.ps(ph, "apN", [128, 512], F32)])
                for st in range(S // ST):
                    T0 = st * ST
                    for tab, src_t in ((cosT, self.rope_cos), (sinT, self.rope_sin)):
                        for hl in range(4):
                            ta = tab.ap
                            dst = bass.AP(ta.tensor, ta.offset + 32 * hl * ta.ap[0][0],
                                          [[ta.ap[0][0], 32], [JW, d], [1, JW]])
                            sa = src_t[g]
                            sr = bass.AP(sa.tensor, sa.offset + T0 // d, [[S, 32], [Lg, d], [1, JW]])
                            kb.dma("sp", dst, sr, writes=[tab])
                    for sub in range(ST // 128):
                        self.norm_T(src, T0 // 128 + sub, self.gain_a, li, hring, xsring, psT, hn, None, ssring,
                                    perm=(d, JW, sub * (128 // d)))
                    pend = None
                    for t in range(2):
                        Wt = W[t].ap
                        dst_t = self.qT[g] if t == 0 else self.kT[g]
                        for fbp in range(4):
                            stA, stB = stage.next(), stage.next()
                            for tt in range(ST // 512):
                                cols = slice(tt * 512, (tt + 1) * 512)
                                pA, pB = psA.next(), psB.next()
                                for half, ps in ((0, pA), (1, pB)):
                                    for kc in range(8):
                                        lhs = Wt[:, kc, fbp * 256 + half * 128:fbp * 256 + (half + 1) * 128]
                                        kb.op("pe", lambda e, ps=ps, lhs=lhs, kc=kc: e.matmul(
                                            ps[:, :], lhs, hn.ap[:, kc, cols], start=(kc == 0), stop=(kc == 7)),
                                            reads=[W[t], hn], writes=[ps], sig=(kc == 7))
                                sqA, sqB = sqring.next(), sqring.next()
                                kb.op("act", lambda e: e.activation(sqA[:, :], pA[:, :], AF.Square), reads=[pA], writes=[sqA])
                                kb.op("act", lambda e: e.activation(sqB[:, :], pB[:, :], AF.Square), reads=[pB], writes=[sqB])

                                def stage2(t=t, fbp=fbp, tt=tt, cols=cols, pA=pA, pB=pB, sqA=sqA, sqB=sqB, stA=stA, stB=stB, dst_t=dst_t):
                                    pN = psN.next()
                                    kb.op("pe", lambda e: e.matmul(pN[:, :], cbf[:, CB_B32:CB_B32 + 128], sqA[:, :], start=True, stop=False),
                                          reads=[cbf, sqA], writes=[pN])
                                    kb.op("pe", lambda e: e.matmul(pN[:, :], cbf[:, CB_B32:CB_B32 + 128], sqB[:, :], start=False, stop=True),
                                          reads=[cbf, sqB], writes=[pN])
                                    r = rring.next()
                                    kb.op("act", lambda e: e.activation(r[:, :], pN[:, :], AF.Ln, bias=self.cf[:, CF_EPS64:CF_EPS64 + 1]),
                                          reads=[pN, self.cf], writes=[r])
                                    kb.op("act", lambda e: e.activation(r[:, :], r[:, :], AF.Exp, scale=-0.5,
                                                                        bias=self.cf[:, CF_LN8 + (1 - t):CF_LN8 + (1 - t) + 1]),
                                          reads=[r, self.cf], writes=[r])
                                    a_, b_ = abring.next(), abring.next()
                                    gA = qkg[:, g * 4 + t * 2:g * 4 + t * 2 + 1]
                                    gB = qkg[:, g * 4 + t * 2 + 1:g * 4 + t * 2 + 2]
                                    kb.op("dve", lambda e: e.scalar_tensor_tensor(a_[:, :], pA[:, :], gA, r[:, :], ALU.mult, ALU.mult),
                                          reads=[pA, qkg, r], writes=[a_])
                                    kb.op("dve", lambda e: e.scalar_tensor_tensor(b_[:, :], pB[:, :], gB, r[:, :], ALU.mult, ALU.mult),
                                          reads=[pB, qkg, r], writes=[b_])
                                    t1, t2 = tmpd.next(), tmpd.next()
                                    t3, t4 = tmpp.next(), tmpp.next()
                                    kb.op("dve", lambda e: e.tensor_tensor(t1[:, :], a_[:, :], cosT[:, cols], ALU.mult), reads=[a_, cosT], writes=[t1])
                                    kb.op("dve", lambda e: e.tensor_tensor(t2[:, :], b_[:, :], sinT[:, cols], ALU.mult), reads=[b_, sinT], writes=[t2])
                                    kb.op("dve", lambda e: e.tensor_tensor(stA[:, cols], t1[:, :], t2[:, :], ALU.subtract), reads=[t1, t2], writes=[stA])
                                    kb.op("pool", lambda e: e.tensor_tensor(t3[:, :], b_[:, :], cosT[:, cols], ALU.mult), reads=[b_, cosT], writes=[t3])
                                    kb.op("pool", lambda e: e.tensor_tensor(t4[:, :], a_[:, :], sinT[:, cols], ALU.mult), reads=[a_, sinT], writes=[t4])
                                    kb.op("pool", lambda e: e.tensor_tensor(stB[:, cols], t3[:, :], t4[:, :], ALU.add), reads=[t3, t4], writes=[stB])
                                    if tt == ST // 512 - 1:
                                        for half, stx in ((0, stA), (1, stB)):
                                            da = dst_t
                                            dap = bass.AP(da.tensor, da.offset + half * 512 * S + fbp * 128 * S + T0 // d,
                                                          [[S, 128], [Lg, d], [1, JW]])
                                            sx = stx.ap
                                            sap = bass.AP(sx.tensor, sx.offset, [list(sx.ap[0]), [JW, d], [1, JW]])
                                            kb.dma("pool", dap, sap, reads=[stx])

                                if pend is not None:
                                    pend()
                                pend = stage2
                    if pend is not None:
                        pend()
                        pend = None
                    for tsub in range(ST // 128):
                        vs = vst.next()
                        for half in range(2):
                            pv = psV.next()
                            for kc in range(8):
                                kb.op("pe", lambda e, kc=kc: e.matmul(pv[:, :], hn.ap[:, kc, tsub * 128:(tsub + 1) * 128],
                                                                      W[2].ap[:, kc, half * 512:(half + 1) * 512],
                                                                      start=(kc == 0), stop=(kc == 7)),
                                      reads=[hn, W[2]], writes=[pv], sig=(kc == 7))
                            kb.op("act", lambda e: e.copy(vs[:, half * 512:(half + 1) * 512], pv[:, :]), reads=[pv], writes=[vs])
                        p = (tsub * 128) // JW
                        jj0 = (tsub * 128) % JW
                        row0 = p * Lg + T0 // d + jj0
                        kb.dma("pool", self.vv[g][row0:row0 + 128, :], vs[:, :], reads=[vs])
                kb.barrier()

        ST2 = min(2048, S)
        with ExitStack() as ph:
            accR = kb.sb_ring(ph, "acc", [128, ST2], F32, 3)
            amk = kb.sb(ph, "amk", [128, 256], BF16)
            kb.dma("sp", amk[:, :], self.consts_bf[:, CB_M1F:CB_M1F + 256], writes=[amk])
            shf = kb.sb(ph, "ashf", [128, 64], F32)
            kb.dma("sp", shf[:, :], self.consts_f[:, 516:580], writes=[shf])
            kbuf = kb.sb_ring(ph, "akb", [64, 2 * ST2], BF16, 4)
            qbuf = kb.sb_ring(ph, "aqb", [64, ST2], BF16, 4)
            vbuf = kb.sb_ring(ph, "avb", [128, 4096], BF16, 4)
            for t_ in kbuf.tiles:
                kb.op("pool", lambda e, t_=t_: e.memset(t_[:, :], 0.0), writes=[t_])
            for t_ in vbuf.tiles:
                kb.op("pool", lambda e, t_=t_: e.memset(t_[:, :], 0.0), writes=[t_])
                kb.op("pool", lambda e, t_=t_: e.memset(t_.ap.rearrange("p (s c) -> p s c", c=128)[:, :, 64:128], 1.0), writes=[t_])
            Ptr = kb.sb_ring(ph, "aPt", [128, 512], BF16, 8)
            ost = kb.sb_ring(ph, "aos", [64, ST2], BF16, 3)
            rBr = kb.sb_ring(ph, "arB", [64, ST2], F32, 2)
            psS = Ring([kb.ps(ph, f"apS{i}", [128, 512], F32) for i in range(5)])
            psA = Ring([kb.ps(ph, f"apA{i}", [128, 512], F32) for i in range(3)])
            deferred = []
            for st in range(S // ST2):
                T0 = st * ST2
                for h in range(16):
                    acc = accR.next()
                    pending = []
                    for g, (win, d) in enumerate(A_GROUPS):
                        Lg = S // d
                        JW2 = ST2 // d
                        NQ = JW2 // 128
                        NKT = NQ + 1
                        KW = JW2 + 128
                        j0 = T0 // d
                        lo_clip = (j0 == 0)
                        hi_clip = (j0 + JW2 == Lg)
                        kt, qt, vt = kbuf.next(), qbuf.next(), vbuf.next()
                        kta, qta, vta = kt.ap, qt.ap, vt.ap
                        kps, vps = kta.ap[0][0], vta.ap[0][0]
                        c_lo = 64 if lo_clip else 0
                        c_hi = KW - 64 if hi_clip else KW
                        for half in range(2):
                            ks = self.kT[g]
                            sap = bass.AP(ks.tensor, ks.offset + half * 512 * S + h * 32 * S + (j0 - 64 + c_lo),
                                          [[S, 32], [Lg, d], [1, c_hi - c_lo]])
                            dap = bass.AP(kta.tensor, kta.offset + 32 * half * kps + c_lo, [[kps, 32], [KW, d], [1, c_hi - c_lo]])
                            kb.dma("sp", dap, sap, writes=[kt])
                            qs = self.qT[g]
                            sap = bass.AP(qs.tensor, qs.offset + half * 512 * S + h * 32 * S + j0, [[S, 32], [Lg, d], [1, JW2]])
                            dap = bass.AP(qta.tensor, qta.offset + 32 * half * qta.ap[0][0], [[qta.ap[0][0], 32], [JW2, d], [1, JW2]])
                            kb.dma("sp", dap, sap, writes=[qt])
                        vs = self.vv[g]
                        i_lo = 1 if lo_clip else 0
                        i_hi = NKT - 1 if hi_clip else NKT
                        if lo_clip:
                            sap = bass.AP(vs.tensor, vs.offset + h * 64, [[D, 64], [Lg * D, d], [1, 64]])
                            dap = bass.AP(vta.tensor, vta.offset + 64 * vps, [[vps, 64], [NKT * 128, d], [1, 64]])
                            kb.dma("sp", dap, sap, writes=[vt])
                        if hi_clip:
                            sap = bass.AP(vs.tensor, vs.offset + (Lg - 64) * D + h * 64, [[D, 64], [Lg * D, d], [1, 64]])
                            dap = bass.AP(vta.tensor, vta.offset + (NKT - 1) * 128, [[vps, 64], [NKT * 128, d], [1, 64]])
                            kb.dma("sp", dap, sap, writes=[vt])
                        if i_hi > i_lo:
                            ni = i_hi - i_lo
                            if ni == 1 or d == 1:
                                dims_s = [[D, 128], [Lg * D, d], [1, 64]] if ni == 1 else [[D, 128], [128 * D, ni], [1, 64]]
                                dims_d = [[vps, 128], [NKT * 128, d], [1, 64]] if ni == 1 else [[vps, 128], [128, ni], [1, 64]]
                                sap = bass.AP(vs.tensor, vs.offset + (j0 - 64 + 128 * i_lo) * D + h * 64, dims_s)
                                dap = bass.AP(vta.tensor, vta.offset + i_lo * 128, dims_d)
                                kb.dma("sp", dap, sap, writes=[vt])
                            else:
                                for p in range(d):
                                    sap = bass.AP(vs.tensor, vs.offset + (p * Lg + j0 - 64 + 128 * i_lo) * D + h * 64,
                                                  [[D, 128], [128 * D, ni], [1, 64]])
                                    dap = bass.AP(vta.tensor, vta.offset + (p * NKT + i_lo) * 128, [[vps, 128], [128, ni], [1, 64]])
                                    kb.dma("sp", dap, sap, writes=[vt])
                        if NQ >= 4:
                            units = [[(p, q0 + k) for k in range(4)] for p in range(d) for q0 in range(0, NQ, 4)]
                        else:
                            assert NQ == 1 and d % 4 == 0
                            units = [[(p0 + k, 0) for k in range(4)] for p0 in range(0, d, 4)]

                        def stage1(unit, kt=kt, qt=qt, KW=KW, JW2=JW2, lo_clip=lo_clip, hi_clip=hi_clip, NQ=NQ):
                            Pts = []
                            for b2 in range(2):
                                ps = psS.next()
                                for ql in range(2):
                                    p, qi = unit[2 * b2 + ql]
                                    for ti in range(2):
                                        kcol = p * KW + (qi + ti) * 128
                                        oc = (2 * ql + ti) * 128
                                        kb.op("pe", lambda e, kcol=kcol, oc=oc, qi=qi, p=p: e.matmul(
                                            ps[:, oc:oc + 128], kt[:, kcol:kcol + 128],
                                            qt[:, p * JW2 + qi * 128:p * JW2 + (qi + 1) * 128], start=True, stop=True),
                                            reads=[kt, qt], writes=[ps], sig=(ql == 1 and ti == 1))
                                Pt = Ptr.next()
                                kb.op("act", lambda e, Pt=Pt, ps=ps: e.activation(Pt[:, :], ps[:, :], AF.Exp),
                                      reads=[ps], writes=[Pt])
                                ca = cbf.ap
                                clipped = [(lo_clip and unit[2 * b2 + ql][1] == 0, hi_clip and unit[2 * b2 + ql][1] == NQ - 1)
                                           for ql in range(2)]
                                if not any(c_ for pr_ in clipped for c_ in pr_):
                                    msk = bass.AP(ca.tensor, ca.offset + CB_M1, [list(ca.ap[0]), [0, 2], [1, 256]])
                                    ptv = Pt.ap.rearrange("p (a b) -> p a b", a=2)
                                    kb.op(("pool" if b2 == 0 else "dve"), lambda e, ptv=ptv, msk=msk: e.tensor_tensor(ptv, ptv, msk, ALU.mult),
                                          reads=[Pt, cbf], writes=[Pt])
                                else:
                                    for ql in range(2):
                                        for ti in range(2):
                                            mt, mc = cbf, (CB_M1, CB_M2)[ti]
                                            if ti == 0 and clipped[ql][0]:
                                                mt, mc = amk, 0
                                            if ti == 1 and clipped[ql][1]:
                                                mt, mc = amk, 128
                                            oc = (2 * ql + ti) * 128
                                            kb.op("pool", lambda e, oc=oc, mc=mc, mt=mt, Pt=Pt: e.tensor_tensor(
                                                Pt[:, oc:oc + 128], Pt[:, oc:oc + 128], mt[:, mc:mc + 128], ALU.mult),
                                                reads=[Pt, mt], writes=[Pt])
                                Pts.append(Pt)
                            return Pts

                        def stage2(unit, Pts, vt=vt, NKT=NKT, d=d, g=g, acc=acc):
                            pa = psA.next()
                            for ql in range(4):
                                p, qi = unit[ql]
                                Pt = Pts[ql // 2]
                                off = (ql % 2) * 256
                                for ti in range(2):
                                    vcol = (p * NKT + qi + ti) * 128
                                    kb.op("pe", lambda e, vcol=vcol, Pt=Pt, off=off, ti=ti, ql=ql: e.matmul(
                                        pa[:, ql * 128:(ql + 1) * 128], vt[:, vcol:vcol + 128],
                                        Pt[:, off + ti * 128:off + (ti + 1) * 128], start=(ti == 0), stop=(ti == 1)),
                                        reads=[vt, Pt], writes=[pa], sig=(ql == 3 and ti == 1))
                            p0, qi0 = unit[0]
                            aa = acc.ap
                            if unit[1][0] == p0:
                                oap = bass.AP(aa.tensor, aa.offset + (qi0 * 128) * d + p0, [list(aa.ap[0]), [d, 512]])
                                iap = pa[:, :]
                            else:
                                oap = bass.AP(aa.tensor, aa.offset + (qi0 * 128) * d + p0, [list(aa.ap[0]), [1, 4], [d, 128]])
                                iap = pa.ap.rearrange("p (a b) -> p a b", a=4)
                            if g == 0:
                                kb.op("act", lambda e, oap=oap, iap=iap: e.copy(oap, iap), reads=[pa], writes=[acc])
                            else:
                                kb.op("dve", lambda e, oap=oap, iap=iap: e.tensor_tensor(oap, iap, oap, ALU.add),
                                      reads=[pa, acc], writes=[acc])

                        for unit in units:
                            Pts = stage1(unit)
                            pending.append((stage2, unit, Pts))
                            if len(pending) > 2:
                                f_, u_, p_ = pending.pop(0)
                                f_(u_, p_)
                            for df in deferred:
                                df[0] -= 1
                            while deferred and deferred[0][0] <= 0:
                                deferred.pop(0)[1]()
                    while pending:
                        f_, u_, p_ = pending.pop(0)
                        f_(u_, p_)
                    def finish(h=h, T0=T0, acc=acc):
                        rb, ot = rBr.next(), ost.next()
                        for c4 in range(ST2 // 512):
                            c4s = slice(c4 * 512, (c4 + 1) * 512)
                            pbm = psA.next()
                            kb.op("pe", lambda e: e.matmul(pbm[0:64, :], shf[:, :], acc[:, c4s], start=True, stop=True),
                                  reads=[shf, acc], writes=[pbm])
                            kb.op("act", lambda e: e.activation(rb[:, c4s], pbm[0:64, :], AF.Ln), reads=[pbm], writes=[rb])
                            kb.op("act", lambda e: e.activation(rb[:, c4s], rb[:, c4s], AF.Exp, scale=-1.0), reads=[rb], writes=[rb])
                            kb.op("dve", lambda e: e.tensor_tensor(ot[:, c4s], acc[0:64, c4s], rb[:, c4s], ALU.mult), reads=[acc, rb], writes=[ot])

                        def store(ot=ot):
                            kb.dma("pool", self.oT[h * 64:(h + 1) * 64, T0:T0 + ST2], ot[:, :], reads=[ot])
                        deferred.append([4, store])

                    deferred.append([4, finish])
            while deferred:
                deferred.pop(0)[1]()
            kb.barrier()

        with ExitStack() as ph:
            Wo = kb.sb(ph, "awo", [128, 8, D], BF16)
            with ExitStack() as ld:
                stg = kb.sb_ring(ld, "awst", [128, 2048], F32, 3)
                for kc in range(8):
                    self.load_w_bf16(None, Wo, lambda cc, cw, kc=kc: Wo.ap[:, kc, cc:cc + cw],
                                     self.a_w_out[ja, kc * 128:(kc + 1) * 128, :], D, stg,
                                     cast_eng=("pool" if kc % 2 == 0 else "act"))
                kb.barrier()
            self.out_proj(ph, Wo, self.oT, src)
            kb.barrier()

    def out_proj(self, ph, Wo, yT_dram, src):
        kb, S = self.kb, self.S
        otr = kb.sb_ring(ph, "opy", [128, 8, 512], BF16, 2)
        hring = kb.sb_ring(ph, "oph", [128, D], F32, 4)
        psO = Ring([kb.ps(ph, f"opp{i}", [128, 512], F32) for i in range(4)])
        for tt in range(S // 512):
            ott = otr.next()
            kb.dma("sp", ott[:, :, :], yT_dram[:, tt * 512:(tt + 1) * 512].rearrange("(f p) t -> p f t", p=128), writes=[ott])
            for sub in range(4):
                ti = tt * 4 + sub
                ht = hring.next()
                rd = [self.h_tiles[ti]] if src is self.h else []
                kb.dma("sp", ht[:, :], src[ti * 128:(ti + 1) * 128, :], reads=rd, writes=[ht])
                for half in range(2):
                    po = psO.next()
                    for fc in range(8):
                        kb.op("pe", lambda e, fc=fc: e.matmul(po[:, :], ott.ap[:, fc, sub * 128:(sub + 1) * 128],
                                                              Wo.ap[:, fc, half * 512:(half + 1) * 512],
                                                              start=(fc == 0), stop=(fc == 7)),
                              reads=[ott, Wo], writes=[po], sig=(fc == 7))
                    kb.op("dve", lambda e: e.tensor_tensor(ht[:, half * 512:(half + 1) * 512],
                                                           ht[:, half * 512:(half + 1) * 512], po[:, :], ALU.add),
                          reads=[ht, po], writes=[ht])
                kb.dma("pool", self.h[ti * 128:(ti + 1) * 128, :], ht[:, :], reads=[ht], writes=[self.h_tiles[ti]])

    def gla_layer(self, li, jb, src):
        kb, S = self.kb, self.S
        cbf, cf = self.cbf, self.cf
        NCH = S // 128
        QS = 128 ** -0.5
        with ExitStack() as lay:
            dec = [kb.sb(lay, f"gdec{dr}", [128, 4, NCH], F32) for dr in range(2)]
            tri = kb.sb(lay, "gtri", [128, 512], F32)
            kb.dma("sp", tri[:, :], self.consts_f[:, 0:512], writes=[tri])
            gmsk = kb.sb(lay, "gmsk", [128, 256], BF16)
            kb.dma("sp", gmsk[:, :], self.consts_bf[:, CB_GMF:CB_GMF + 256], writes=[gmsk])
            with ExitStack() as ph:
                Wb = kb.sb(ph, "gw", [128, 8, 3104], BF16)
                wg = [kb.sb(ph, f"gwg{dr}", [33, 512], BF16) for dr in range(2)]
                with ExitStack() as ld:
                    stg = kb.sb_ring(ld, "gwst", [128, 2048], F32, 3)
                    for kc in range(8):
                        self.load_w_bf16(None, Wb, lambda cc, cw, kc=kc: Wb.ap[:, kc, cc:cc + cw],
                                         self.b_w_in[jb, kc * 128:(kc + 1) * 128, :], 3104, stg,
                                         cast_eng=("pool" if kc % 2 == 0 else "act"))
                    gst = kb.sb_ring(ld, "ggst", [33, 512], F32, 2)
                    for dr in range(2):
                        g_ = gst.next()
                        kb.op("pool", lambda e: e.memset(g_[:, :], 0.0), writes=[g_])
                        kb.dma("sp", g_[0:16, :], self.b_w_gate[dr][jb], writes=[g_])
                        kb.dma("sp", g_[32:33, :], self.b_gate_bias[dr][jb:jb + 1, :], writes=[g_])
                        kb.op("pool", lambda e: e.tensor_copy(wg[dr][:, :], g_[:, :]), reads=[g_], writes=[wg[dr]])
                    kb.barrier()
                hring = kb.sb_ring(ph, "gh", [128, D], F32, 3)
                xsring = kb.sb_ring(ph, "gxs", [128, D], BF16, 2)
                ssring = kb.sb_ring(ph, "gss", [128, 4], F32, 4)
                hnring = kb.sb_ring(ph, "ghn", [128, 8, 512], BF16, 2)
                zT = [kb.sb_ring(ph, f"gz{dr}", [33, 512], BF16, 2) for dr in range(2)]
                for dr in range(2):
                    for z in zT[dr].tiles:
                        kb.op("pool", lambda e, z=z: e.memset(z[:, :], 0.0), writes=[z])
                        kb.op("pool", lambda e, z=z: e.memset(z[32:33, :], 1.0), writes=[z])
                ltr = kb.sb_ring(ph, "gl", [128, 512], F32, 4)
                ET = [[kb.sb(ph, f"gE{dr}{sg}", [128, 4, 512], F32) for sg in range(2)] for dr in range(2)]
                Egr = kb.sb_ring(ph, "gEg", [128, 512], F32, 3)
                oqr = kb.sb_ring(ph, "goq", [128, 512], BF16, 4)
                ker = kb.sb_ring(ph, "gke", [128, 512], BF16, 3)
                vor = kb.sb_ring(ph, "gvo", [128, D], BF16, 2)
                sro = kb.sb_ring(ph, "gsr", [128, D], BF16, 2)
                tmr = kb.sb_ring(ph, "gtm", [128, 512], F32, 2)
                psT = Ring([kb.ps(ph, "gpT", [128, 512], F32)])
                psP = Ring([kb.ps(ph, f"gpP{i}", [128, 512], F32) for i in range(3)])
                psC = Ring([kb.ps(ph, f"gpC{i}", [128, 512], F32) for i in range(3)])
                psZ = Ring([kb.ps(ph, "gpZ", [128, 512], F32)])
                tri_cum = (CF_TRI_LE, CF_TRI_GE)
                tri_end = (CF_TRI_GT, CF_TRI_LT)
                for tt in range(S // 512):
                    hn = hnring.next()
                    for sub in range(4):
                        self.norm_T(src, tt * 4 + sub, self.gain_a, li, hring, xsring, psT, hn, sub * 128, ssring)
                    zt = [zT[0].next(), zT[1].next()]
                    for dr in range(2):
                        pz = psZ.next()
                        for kc in range(8):
                            kb.op("pe", lambda e, kc=kc: e.matmul(pz[0:16, :], Wb.ap[:, kc, 3072 + 16 * dr:3088 + 16 * dr],
                                                                  hn.ap[:, kc, :], start=(kc == 0), stop=(kc == 7)),
                                  reads=[Wb, hn], writes=[pz], sig=(kc == 7))
                        kb.op("act", lambda e: e.copy(zt[dr][0:16, :], pz[0:16, :]), reads=[pz], writes=[zt[dr]])
                    for sub in range(4):
                        cs = slice(sub * 128, (sub + 1) * 128)
                        chunk = tt * 4 + sub
                        r0 = chunk * 128
                        lts = []
                        for dr in range(2):
                            pzz = psC.next()
                            kb.op("pe", lambda e: e.matmul(pzz[:, :], zt[dr][0:33, cs], wg[dr][0:33, :], start=True, stop=True),
                                  reads=[zt[dr], wg[dr]], writes=[pzz])
                            lt = ltr.next()
                            kb.op("act", lambda e: e.activation(lt[:, :], pzz[:, :], AF.Exp, scale=-1.0), reads=[pzz], writes=[lt])
                            kb.op("act", lambda e: e.activation(lt[:, :], lt[:, :], AF.Ln, bias=cf[:, CF_ONE:CF_ONE + 1]),
                                  reads=[lt, cf], writes=[lt])
                            lts.append(lt)
                        egs = []
                        for dr in range(2):
                            lt = lts[dr]
                            pc = psC.next()
                            for hq in range(4):
                                kb.op("pe", lambda e, hq=hq: e.matmul(pc[:, hq * 128:(hq + 1) * 128], lt[:, hq * 128:(hq + 1) * 128],
                                                                      tri[:, tri_cum[dr]:tri_cum[dr] + 128], start=True, stop=True),
                                      reads=[lt, tri], writes=[pc])
                            pcv = pc.ap.rearrange("p (h c) -> p h c", h=4)
                            kb.op("act", lambda e: e.activation(ET[dr][0].ap[:, :, cs], pcv, AF.Exp), reads=[pc], writes=[ET[dr][0]])
                            kb.op("act", lambda e: e.activation(ET[dr][1].ap[:, :, cs], pcv, AF.Exp, scale=-1.0), reads=[pc], writes=[ET[dr][1]])
                            ecol = sub * 128 + (127 if dr == 0 else 0)
                            kb.op("dve", lambda e: e.tensor_copy(dec[dr].ap[:, :, chunk], ET[dr][0].ap[:, :, ecol]),
                                  reads=[ET[dr][0]], writes=[dec[dr]])
                            pg = psC.next()
                            kb.op("pe", lambda e: e.matmul(pg[:, :], tri[:, tri_end[dr]:tri_end[dr] + 128], lt[:, :], start=True, stop=True),
                                  reads=[tri, lt], writes=[pg])
                            eg = Egr.next()
                            kb.op("act", lambda e: e.activation(eg[:, :], pg[:, :], AF.Exp), reads=[pg], writes=[eg])
                            egs.append(eg)
                        pk = psP.next()
                        for kc in range(8):
                            kb.op("pe", lambda e, kc=kc: e.matmul(pk[:, :], hn.ap[:, kc, cs], Wb.ap[:, kc, 512:1024],
                                                                  start=(kc == 0), stop=(kc == 7)), reads=[hn, Wb], writes=[pk], sig=(kc == 7))
                        for dr in range(2):
                            ke = ker.next()
                            kb.op("dve", lambda e: e.tensor_tensor(ke[:, :], pk[:, :], egs[dr][:, :], ALU.mult),
                                  reads=[pk, egs[dr]], writes=[ke])
                            kb.dma("pool", self.g_kend[dr][r0:r0 + 128, :], ke[:, :], reads=[ke])
                        vo = vor.next()
                        for half in range(2):
                            pv = psP.next()
                            for kc in range(8):
                                kb.op("pe", lambda e, kc=kc: e.matmul(pv[:, :], hn.ap[:, kc, cs],
                                                                      Wb.ap[:, kc, 1024 + half * 512:1536 + half * 512],
                                                                      start=(kc == 0), stop=(kc == 7)), reads=[hn, Wb], writes=[pv], sig=(kc == 7))
                            kb.op("act", lambda e: e.copy(vo[:, half * 512:(half + 1) * 512], pv[:, :]), reads=[pv], writes=[vo])
                        kb.dma("pool", self.g_v[r0:r0 + 128, :], vo[:, :], reads=[vo])
                        so = sro.next()
                        for half in range(2):
                            pr = psP.next()
                            for kc in range(8):
                                kb.op("pe", lambda e, kc=kc: e.matmul(pr[:, :], hn.ap[:, kc, cs],
                                                                      Wb.ap[:, kc, 2048 + half * 512:2560 + half * 512],
                                                                      start=(kc == 0), stop=(kc == 7)), reads=[hn, Wb], writes=[pr], sig=(kc == 7))
                            tm = tmr.next()
                            kb.op("act", lambda e: e.activation(tm[:, :], pr[:, :], AF.Exp, scale=-1.0), reads=[pr], writes=[tm])
                            kb.op("act", lambda e: e.activation(tm[:, :], tm[:, :], AF.Ln, bias=cf[:, CF_ONE:CF_ONE + 1]),
                                  reads=[tm, cf], writes=[tm])
                            kb.op("act", lambda e: e.activation(tm[:, :], tm[:, :], AF.Exp, scale=-1.0), reads=[tm], writes=[tm])
                            kb.op("dve", lambda e: e.tensor_tensor(so[:, half * 512:(half + 1) * 512], tm[:, :], pr[:, :], ALU.mult),
                                  reads=[tm, pr], writes=[so])
                        kb.dma("pool", self.g_sr[r0:r0 + 128, :], so[:, :], reads=[so])
                    for kind in range(2):
                        for hq in range(4):
                            pq = psP.next()
                            c0 = kind * 512 + hq * 128
                            for kc in range(8):
                                kb.op("pe", lambda e, kc=kc: e.matmul(pq[:, :], Wb.ap[:, kc, c0:c0 + 128], hn.ap[:, kc, :],
                                                                      start=(kc == 0), stop=(kc == 7)), reads=[Wb, hn], writes=[pq], sig=(kc == 7))
                            for dr in range(2):
                                E = ET[dr][kind]
                                oq = oqr.next()
                                kb.op("dve", lambda e: e.scalar_tensor_tensor(oq[:, :], pq[:, :], (QS if kind == 0 else 1.0),
                                                                              E.ap[:, hq, :], ALU.mult, ALU.mult),
                                      reads=[pq, E], writes=[oq])
                                dst = (self.g_qt if kind == 0 else self.g_kt)[dr]
                                kb.dma("pool", dst[hq * 128:(hq + 1) * 128, tt * 512:(tt + 1) * 512], oq[:, :], reads=[oq])
                kb.barrier()

            with ExitStack() as ph:
                Sst = [[kb.sb(ph, f"gS{dr}{hq}", [128, 256], F32) for hq in range(4)] for dr in range(2)]
                Sbf = [[kb.sb(ph, f"gSb{dr}{hq}", [128, 256], BF16) for hq in range(4)] for dr in range(2)]
                for dr in range(2):
                    for hq in range(4):
                        kb.op("pool", lambda e: e.memset(Sst[dr][hq][:, :], 0.0), writes=[Sst[dr][hq]])
                        kb.op("pool", lambda e: e.memset(Sbf[dr][hq][:, :], 0.0), writes=[Sbf[dr][hq]])
                qbr = [kb.sb_ring(ph, f"gqb{dr}", [128, 4, 512], BF16, 2) for dr in range(2)]
                kbr = [kb.sb_ring(ph, f"gkb{dr}", [128, 4, 512], BF16, 2) for dr in range(2)]
                ker = [kb.sb_ring(ph, f"gkeb{dr}", [128, 4, 512], BF16, 2) for dr in range(2)]
                vbr = [kb.sb_ring(ph, f"gvb{dr}", [128, 4, D], BF16, 2) for dr in range(2)]
                amr = kb.sb_ring(ph, "gam", [128, 512], BF16, 4)
                ostr = kb.sb_ring(ph, "gos", [128, D], F32, 4)
                psAt = Ring([kb.ps(ph, f"gpA{i}", [128, 512], F32) for i in range(3)])
                psO = Ring([kb.ps(ph, f"gpO{i}", [128, 512], F32) for i in range(3)])
                psKV = Ring([kb.ps(ph, f"gpK{i}", [128, 512], F32) for i in range(2)])
                NB = S // 512
                pendg = []
                for nb in range(NB):
                    blkt = []
                    for dr in range(2):
                        blk = nb if dr == 0 else NB - 1 - nb
                        ts = slice(blk * 512, (blk + 1) * 512)
                        qb_, kb_, ke_, vb_ = qbr[dr].next(), kbr[dr].next(), ker[dr].next(), vbr[dr].next()
                        kb.dma("sp", qb_[:, :, :], self.g_qt[dr][:, ts].rearrange("(h p) t -> p h t", p=128), writes=[qb_])
                        kb.dma("sp", kb_[:, :, :], self.g_kt[dr][:, ts].rearrange("(h p) t -> p h t", p=128), writes=[kb_])
                        kb.dma("sp", ke_[:, :, :], self.g_kend[dr][ts, :].rearrange("(c p) f -> p c f", p=128), writes=[ke_])
                        kb.dma("sp", vb_[:, :, :], self.g_v[ts, :].rearrange("(c p) f -> p c f", p=128), writes=[vb_])
                        blkt.append((blk, qb_, kb_, ke_, vb_))
                    for ci in range(4):
                        for dr in range(2):
                            blk, qb_, kb_, ke_, vb_ = blkt[dr]
                            c = ci if dr == 0 else 3 - ci
                            cs = slice(c * 128, (c + 1) * 128)
                            gch = blk * 4 + c
                            mcol = 0 if dr == 0 else 128
                            pat = psAt.next()
                            for hq in range(4):
                                kb.op("pe", lambda e, hq=hq: e.matmul(pat[:, hq * 128:(hq + 1) * 128], kb_.ap[:, hq, cs],
                                                                      qb_.ap[:, hq, cs], start=True, stop=True),
                                      reads=[kb_, qb_], writes=[pat])
                            am = amr.next()
                            ca = gmsk.ap
                            msk = bass.AP(ca.tensor, ca.offset + mcol, [list(ca.ap[0]), [0, 4], [1, 128]])
                            kb.op("dve", lambda e, am=am, pat=pat, msk=msk: e.tensor_tensor(
                                am.ap.rearrange("p (a b) -> p a b", a=4), pat.ap.rearrange("p (a b) -> p a b", a=4), msk, ALU.mult),
                                reads=[pat, gmsk], writes=[am])

                            def stageB(dr=dr, qb_=qb_, ke_=ke_, vb_=vb_, c=c, cs=cs, gch=gch, am=am):
                                ot = ostr.next()
                                for hq in range(4):
                                    vs_ = vb_.ap[:, c, hq * 256:(hq + 1) * 256]
                                    po = psO.next()
                                    kb.op("pe", lambda e: e.matmul(po[:, 0:256], am[:, hq * 128:(hq + 1) * 128], vs_, start=True, stop=False),
                                          reads=[am, vb_], writes=[po])
                                    kb.op("pe", lambda e: e.matmul(po[:, 0:256], qb_.ap[:, hq, cs], Sbf[dr][hq][:, :], start=False, stop=True),
                                          reads=[qb_, Sbf[dr][hq]], writes=[po])
                                    pkv = psKV.next()
                                    kb.op("pe", lambda e: e.matmul(pkv[:, 0:256], ke_.ap[:, c, hq * 128:(hq + 1) * 128], vs_, start=True, stop=True),
                                          reads=[ke_, vb_], writes=[pkv])
                                    St = Sst[dr][hq]
                                    kb.op("dve", lambda e: e.scalar_tensor_tensor(St[:, :], St[:, :], dec[dr].ap[:, hq, gch:gch + 1],
                                                                                  pkv[:, 0:256], ALU.mult, ALU.add),
                                          reads=[St, dec[dr], pkv], writes=[St])
                                    kb.op("act", lambda e: e.copy(Sbf[dr][hq][:, :], St[:, :]), reads=[St], writes=[Sbf[dr][hq]])
                                    kb.op("act", lambda e: e.copy(ot[:, hq * 256:(hq + 1) * 256], po[:, 0:256]), reads=[po], writes=[ot])
                                kb.dma("pool", self.g_o[dr][gch * 128:(gch + 1) * 128, :], ot[:, :], reads=[ot])

                            pendg.append(stageB)
                            if len(pendg) > 2:
                                pendg.pop(0)()
                while pendg:
                    pendg.pop(0)()
                kb.barrier()

        with ExitStack() as ph:
            gbc = kb.sb(ph, "ggb", [128, D], F32)
            ga = self.b_out_norm[jb]
            kb.dma("sp", gbc[:, :], bass.AP(ga.tensor, ga.offset, [[0, 128], [1, D]]), writes=[gbc])
            ofr = kb.sb_ring(ph, "gof", [128, D], F32, 2)
            obr = kb.sb_ring(ph, "gob", [128, D], F32, 2)
            srr = kb.sb_ring(ph, "gsi", [128, D], BF16, 2)
            sqr = kb.sb_ring(ph, "gsq", [128, 256], BF16, 2)
            ssr = kb.sb_ring(ph, "gs4", [128, 16], F32, 3)
            ybr = kb.sb_ring(ph, "gyb", [128, D], BF16, 2)
            ytr = kb.sb_ring(ph, "gyt", [128, 8, 128], BF16, 3)
            psT = Ring([kb.ps(ph, f"gpT{i}", [128, 512], F32) for i in range(2)])
            for ti in range(S // 128):
                r0 = ti * 128
                of, ob, sr = ofr.next(), obr.next(), srr.next()
                kb.dma("sp", of[:, :], self.g_o[0][r0:r0 + 128, :], writes=[of])
                kb.dma("sp", ob[:, :], self.g_o[1][r0:r0 + 128, :], writes=[ob])
                kb.dma("sp", sr[:, :], self.g_sr[r0:r0 + 128, :], writes=[sr])
                kb.op("pool", lambda e: e.tensor_tensor(of[:, :], of[:, :], ob[:, :], ALU.add), reads=[of, ob], writes=[of])
                ss = ssr.next()
                kb.op("dve", lambda e: e.memset(ss[:, :], 0.0), writes=[ss])
                for hq in range(4):
                    sq = sqr.next()
                    kb.op("act", lambda e, hq=hq, sq=sq: e.activation(sq[:, :], of[:, hq * 256:(hq + 1) * 256], AF.Square,
                                                                      accum_out=ss[:, hq:hq + 1]), reads=[of, ss], writes=[sq, ss])
                kb.op("dve", lambda e: e.tensor_scalar(ss[:, 4:8], ss[:, 0:4], 1.0 / 256, EPS, ALU.mult, ALU.add), reads=[ss], writes=[ss])
                kb.op("act", lambda e: e.activation(ss[:, 8:12], ss[:, 4:8], AF.Ln), reads=[ss], writes=[ss])
                kb.op("act", lambda e: e.activation(ss[:, 12:16], ss[:, 8:12], AF.Exp, scale=-0.5), reads=[ss], writes=[ss])
                for hq in range(4):
                    kb.op("act", lambda e, hq=hq: e.activation(ob[:, hq * 256:(hq + 1) * 256], of[:, hq * 256:(hq + 1) * 256], AF.Copy,
                                                               scale=ss[:, 12 + hq:13 + hq]), reads=[of, ss], writes=[ob])
                kb.op("dve", lambda e: e.tensor_tensor(ob[:, :], ob[:, :], gbc[:, :], ALU.mult), reads=[ob, gbc], writes=[ob])
                yb = ybr.next()
                kb.op("dve", lambda e: e.tensor_tensor(yb[:, :], ob[:, :], sr[:, :], ALU.mult), reads=[ob, sr], writes=[yb])
                pt = psT.next()
                ptb = pt.ap.bitcast(BF16)
                for kc in range(8):
                    kb.op("pe", lambda e, kc=kc: e.transpose(ptb[:, kc * 128:(kc + 1) * 128], yb[:, kc * 128:(kc + 1) * 128],
                                                              self.ident[:, :]), reads=[yb, self.ident], writes=[pt], sig=(kc == 7))
                yt = ytr.next()
                kb.op("act", lambda e: e.copy(yt.ap.rearrange("p f t -> p (f t)"), ptb), reads=[pt], writes=[yt])
                kb.dma("pool", self.g_yT[:, r0:r0 + 128].rearrange("(f p) t -> p f t", p=128), yt[:, :, :], reads=[yt])
            kb.barrier()
        with ExitStack() as ph:
            Wo = kb.sb(ph, "gwo", [128, 8, D], BF16)
            with ExitStack() as ld:
                stg = kb.sb_ring(ld, "gwst2", [128, 2048], F32, 3)
                for kc in range(8):
                    self.load_w_bf16(None, Wo, lambda cc, cw, kc=kc: Wo.ap[:, kc, cc:cc + cw],
                                     self.b_w_out[jb, kc * 128:(kc + 1) * 128, :], D, stg,
                                     cast_eng=("pool" if kc % 2 == 0 else "act"))
                kb.barrier()
            self.out_proj(ph, Wo, self.g_yT, src)
            kb.barrier()

    def ffn_phase(self, li, src, dst):
        kb, nc, S = self.kb, self.nc, self.S
        TT = 512
        with ExitStack() as ph:
            wgu = kb.sb(ph, "wgu", [128, 8, 2 * FH], BF16)
            wdn = kb.sb(ph, "wdn", [128, NJ, D], BF16)
            with ExitStack() as ld:
                stg = kb.sb_ring(ld, "wstg", [128, 2048], F32, 3)
                for kc in range(8):
                    self.load_w_bf16(None, wgu, lambda c0, cw, kc=kc: wgu.ap[:, kc, c0:c0 + cw],
                                     self.w_gu[li, kc * 128:(kc + 1) * 128, :], 2 * FH, stg,
                                     cast_eng=("pool" if kc % 2 == 0 else "act"))
                for j in range(NJ):
                    self.load_w_bf16(None, wdn, lambda c0, cw, j=j: wdn.ap[:, j, c0:c0 + cw],
                                     self.w_dn[li, j * 128:(j + 1) * 128, :], D, stg,
                                     cast_eng=("pool" if j % 2 == 0 else "act"))
                kb.barrier()
            hring = kb.sb_ring(ph, "fh", [128, D], F32, 8)
            xsring = kb.sb_ring(ph, "fxs", [128, D], BF16, 1)
            ssring = kb.sb_ring(ph, "fss", [128, 4], F32, 4)
            hnring = kb.sb_ring(ph, "fhn", [128, 8, TT], BF16, 2)
            actT = kb.sb(ph, "factT", [128, NJ, TT], BF16)
            sgring = kb.sb_ring(ph, "fsg", [128, TT], F32, 1)
            psT = Ring([kb.ps(ph, f"fpT{i}", [128, 512], F32) for i in range(1)])
            psG = Ring([kb.ps(ph, f"fpG{i}", [128, 512], F32) for i in range(2)])
            psU = Ring([kb.ps(ph, f"fpU{i}", [128, 512], F32) for i in range(2)])
            psD = Ring([kb.ps(ph, f"fpD{i}", [128, 512], F32) for i in range(3)])
            def do_norm(tt):
                hn_ = hnring.next()
                hts_ = []
                for sub in range(TT // 128):
                    hts_.append(self.norm_T(src, tt * (TT // 128) + sub, self.gain_f, li, hring, xsring,
                                            psT, hn_, sub * 128, ssring))
                return hn_, hts_

            nxt = do_norm(0)
            for tt in range(S // TT):
                hn, hts = nxt
                for j in range(NJ):
                    pg, pu = psG.next(), psU.next()
                    for kc in range(8):
                        kb.op("pe", lambda e, kc=kc: e.matmul(pg[:, :], wgu.ap[:, kc, j * 128:(j + 1) * 128],
                                                              hn.ap[:, kc, :], start=(kc == 0), stop=(kc == 7)),
                              reads=[wgu, hn], writes=[pg], sig=(kc == 7))
                    for kc in range(8):
                        kb.op("pe", lambda e, kc=kc: e.matmul(pu[:, :], wgu.ap[:, kc, FH + j * 128:FH + (j + 1) * 128],
                                                              hn.ap[:, kc, :], start=(kc == 0), stop=(kc == 7)),
                              reads=[wgu, hn], writes=[pu], sig=(kc == 7))
                    sg = sgring.next()
                    kb.op("act", lambda e: e.activation(sg[:, :], pg[:, :], AF.Exp, scale=-1.0), reads=[pg], writes=[sg])
                    kb.op("act", lambda e: e.activation(sg[:, :], sg[:, :], AF.Ln, bias=self.cf[:, CF_ONE:CF_ONE + 1]),
                          reads=[sg, self.cf], writes=[sg])
                    kb.op("act", lambda e: e.activation(sg[:, :], sg[:, :], AF.Exp, scale=-1.0), reads=[sg], writes=[sg])
                    kb.op("dve", lambda e: e.tensor_tensor(sg[:, :], sg[:, :], pg[:, :], ALU.mult),
                          reads=[sg, pg], writes=[sg])
                    kb.op("dve", lambda e: e.tensor_tensor(actT.ap[:, j, :], sg[:, :], pu[:, :], ALU.mult),
                          reads=[sg, pu], writes=[actT])
                if tt + 1 < S // TT:
                    nxt = do_norm(tt + 1)
                for sub in range(TT // 128):
                    ti = tt * (TT // 128) + sub
                    ot = hts[sub]
                    for half in range(2):
                        pd = psD.next()
                        for j in range(NJ):
                            kb.op("pe", lambda e, j=j: e.matmul(pd[:, :], actT.ap[:, j, sub * 128:(sub + 1) * 128],
                                                                wdn.ap[:, j, half * 512:(half + 1) * 512],
                                                                start=(j == 0), stop=(j == NJ - 1)),
                                  reads=[actT, wdn], writes=[pd], sig=(j == NJ - 1))
                        kb.op("dve", lambda e: e.tensor_tensor(ot[:, half * 512:(half + 1) * 512],
                                                               hts[sub][:, half * 512:(half + 1) * 512], pd[:, :], ALU.add),
                              reads=[hts[sub], pd], writes=[ot])
                    wr = [self.h_tiles[ti]] if dst is self.h else []
                    kb.dma("pool", dst[ti * 128:(ti + 1) * 128, :], ot[:, :], reads=[ot], writes=wr)
            kb.barrier()


def _consts():
    cb = np.zeros((128, NCB), np.float32)
    k = np.arange(128)[:, None]
    m = np.arange(128)[None, :]
    cb[:, CB_B32:CB_B32 + 128] = (k // 32 == m // 32)
    cb[:, CB_M1:CB_M1 + 128] = (k >= m)
    cb[:, CB_M2:CB_M2 + 128] = (k <= m)
    cb[:, CB_ONES:CB_ONES + 64] = 1.0
    cb[64:, CB_ONESF:CB_ONESF + 64] = 1.0
    cb[:64, CB_ONESL:CB_ONESL + 64] = 1.0
    cb[64:, CB_M1F:CB_M1F + 128] = (k >= m)[64:]
    cb[:64, CB_M2L:CB_M2L + 128] = (k <= m)[:64]
    cb[:, CB_GMF:CB_GMF + 128] = (k <= m)
    cb[:, CB_GMB:CB_GMB + 128] = (k > m)
    cf = np.zeros((128, NCF), np.float32)
    w = -1.0 / 16.0
    cf[:, CF_TRI_LE:CF_TRI_LE + 128] = w * (k <= m)
    cf[:, CF_TRI_GE:CF_TRI_GE + 128] = w * (k >= m)
    cf[:, CF_TRI_GT:CF_TRI_GT + 128] = w * (k > m)
    cf[:, CF_TRI_LT:CF_TRI_LT + 128] = w * (k < m)
    cf[:, 512 + CF_EPS64] = 64 * EPS
    cf[:, 512 + CF_LN8] = math.log(8.0)
    cf[:, 512 + CF_ONE] = 1.0
    cf[:, 516:580] = (k == m[:, :64] + 64)
    return cb.astype(ml_dtypes.bfloat16), cf


def _rope_tables(S):
    half = 32
    inv_freq = (10000.0 ** (-np.arange(half, dtype=np.float32) / half)).astype(np.float32)
    cos = np.zeros((3, 32, S), np.float32)
    sin = np.zeros((3, 32, S), np.float32)
    for g, (_, d) in enumerate(A_GROUPS):
        L = S // d
        pos = (np.arange(L, dtype=np.float32)[None, :] * d + np.arange(d, dtype=np.float32)[:, None]).reshape(-1)
        ang = (pos[None, :].astype(np.float32) * inv_freq[:, None]).astype(np.float32)
        cos[g] = np.cos(ang)
        sin[g] = np.sin(ang)
    return cos, sin


def _qk_gain(qn, kn):
    na = qn.shape[0]
    out = np.zeros((na, 128, 12), np.float32)
    for g in range(3):
        for t, v in enumerate((qn, kn)):
            for half in range(2):
                out[:, :, g * 4 + t * 2 + half] = np.tile(v[:, g, half * 32:(half + 1) * 32], (1, 4))
    return out


def _gla_inputs(inp, nb):
    return dict(b_w_in=inp["b_w_in"][:nb], b_w_gate_f=inp["b_w_gate_f"][:nb], b_w_gate_b=inp["b_w_gate_b"][:nb],
                b_gate_bias_f=inp["b_gate_bias_f"][:nb], b_gate_bias_b=inp["b_gate_bias_b"][:nb],
                b_out_norm=np.ascontiguousarray(inp["b_out_norm"][:nb].reshape(nb, D)), b_w_out=inp["b_w_out"][:nb])


def _prep_small(v):
    L = v.shape[0]
    return np.ascontiguousarray(v.reshape(L, 8, 128).transpose(0, 2, 1)).astype(np.float32)


_PROG_CACHE = {}


def kernel(**inputs):
    x = np.asarray(inputs["x"], dtype=np.float32)
    B, S, _ = x.shape
    pat = "abab"
    L = len(pat)
    na, nb = pat.count("a"), pat.count("b")
    key = (S, pat)
    if key not in _PROG_CACHE:
        _PROG_CACHE[key] = Prog(S, list(pat)).build()
    nc = _PROG_CACHE[key]
    f32 = lambda k: np.ascontiguousarray(np.asarray(inputs[k], dtype=np.float32))
    ident = np.eye(128, dtype=np.float32).astype(ml_dtypes.bfloat16)
    cb, cf = _consts()
    cos, sin = _rope_tables(S)
    shared = dict(
        attn_norm=_prep_small(f32("attn_norm")), ffn_norm=_prep_small(f32("ffn_norm")),
        ffn_w_gate_up=f32("ffn_w_gate_up"), ffn_w_down=f32("ffn_w_down"),
        ident=ident, consts_bf=cb, consts_f=cf,
        a_w_in=f32("a_w_in"), a_w_out=f32("a_w_out"),
        qk_gain=_qk_gain(f32("a_q_norm"), f32("a_k_norm")), rope_cos=cos, rope_sin=sin,
    )
    shared.update(_gla_inputs({k: f32(k) for k in ("b_w_in", "b_w_gate_f", "b_w_gate_b", "b_gate_bias_f",
                                                    "b_gate_bias_b", "b_out_norm", "b_w_out")}, nb))
    in_maps = [dict(shared, x=np.ascontiguousarray(x[c])) for c in range(B)]
    res = run_bass_kernel_spmd(nc, in_maps, core_ids=list(range(B)))
    return np.stack([np.asarray(res.results[c]["out"], dtype=np.float32) for c in range(B)], axis=0)
```
